# Optimizing a Trainium2 kernel written in Bass

```python
import jax, jax.numpy as jnp
from jax import lax
import numpy as np

D_MODEL = 1024
BATCH = 16
SEQ = 4096
DEPTH = 2
DEC_BATCH = 4
DEC_SEQ = 8192
PAST_LEN = 128

CONV_WIDTH = D_MODEL // 2
CONV_GROUPS = 8
CONV_K = 31
FOURIER_WIDTH = D_MODEL // 2
FOURIER_GROUPS = 4
FOURIER_GROUP_DIM = FOURIER_WIDTH // FOURIER_GROUPS
SHORT_WIDTH = D_MODEL
SHORT_GROUPS = 16
SHORT_K = 3
EPS = 1e-6
N_EVEN = (DEPTH + 1) // 2
N_ODD = DEPTH // 2
IN_EVEN = 2 * CONV_WIDTH + FOURIER_WIDTH + CONV_WIDTH + FOURIER_WIDTH
IN_ODD = 4 * SHORT_WIDTH
MIX_EVEN = CONV_WIDTH + FOURIER_WIDTH

kernel_name = "hybrid_conformer_fnet_shortconv_encoder"


def rms_norm(x, g):
    xf = x.astype(jnp.float32)
    y = xf * lax.rsqrt(jnp.mean(xf * xf, axis=-1, keepdims=True) + EPS)
    return (y * g.astype(jnp.float32)).astype(x.dtype)


def layer_norm(x, g, b):
    xf = x.astype(jnp.float32)
    mu = jnp.mean(xf, axis=-1, keepdims=True)
    xc = xf - mu
    var = jnp.mean(xc * xc, axis=-1, keepdims=True)
    y = xc * lax.rsqrt(var + EPS) * g.astype(jnp.float32) + b.astype(jnp.float32)
    return y.astype(x.dtype)


def depthwise_conv_centred(x, w, b):
    k, c = w.shape
    pad = (k - 1) // 2
    y = lax.conv_general_dilated(
        x, w[:, None, :].astype(x.dtype),
        window_strides=(1,), padding=[(pad, pad)],
        dimension_numbers=("NWC", "WIO", "NWC"),
        feature_group_count=c)
    return y + b.astype(x.dtype)


def even_layer(x, g_norm, w_in, conv_w, conv_b, ln_g, ln_b, w_four, w_out):
    h = rms_norm(x, g_norm)
    p = h @ w_in
    a_val, a_glu, f_in, a_z, f_z = jnp.split(
        p, [CONV_WIDTH, 2 * CONV_WIDTH, 2 * CONV_WIDTH + FOURIER_WIDTH,
            3 * CONV_WIDTH + FOURIER_WIDTH], axis=-1)
    a = a_val * jax.nn.sigmoid(a_glu)
    a = depthwise_conv_centred(a, conv_w, conv_b)
    a = jax.nn.silu(layer_norm(a, ln_g, ln_b))
    a = a * jax.nn.silu(a_z)
    nb, s, _ = f_in.shape
    fg = f_in.reshape(nb, s, FOURIER_GROUPS, FOURIER_GROUP_DIM)
    fr = jnp.fft.fft2(fg.astype(jnp.float32), axes=(1, 3), norm="ortho").real.astype(x.dtype)
    f = jnp.einsum("bsgc,gcd->bsgd", fr, w_four).reshape(nb, s, FOURIER_WIDTH)
    f = f * jax.nn.silu(f_z)
    y = jnp.concatenate([a, f], axis=-1) @ w_out
    return x + y


def odd_layer(x, g_norm, w_in, conv_w, conv_b, w_out):
    h = rms_norm(x, g_norm)
    p = h @ w_in
    u, bg, cg, z = jnp.split(p, 4, axis=-1)
    v = depthwise_conv_centred(cg * u, conv_w, conv_b)
    y = (bg * v) * jax.nn.silu(z)
    return x + y @ w_out


def trunk(x, ev_norm, ev_w_in, ev_conv_w, ev_conv_b, ev_ln_g, ev_ln_b, ev_w_four, ev_w_out,
          od_norm, od_w_in, od_conv_w, od_conv_b, od_w_out, final_norm):
    for i in range(DEPTH):
        j = i // 2
        if i % 2 == 0:
            x = even_layer(x, ev_norm[j], ev_w_in[j], ev_conv_w[j], ev_conv_b[j],
                           ev_ln_g[j], ev_ln_b[j], ev_w_four[j], ev_w_out[j])
        else:
            x = odd_layer(x, od_norm[j], od_w_in[j], od_conv_w[j], od_conv_b[j], od_w_out[j])
    return rms_norm(x, final_norm)


def setup_inputs(seed: int = 0) -> dict:
    key = jax.random.key(seed)
    ks = jax.random.split(key, 20)
    f32 = jnp.float32
    nrm = lambda k, shape, scale: jax.random.normal(k, shape, f32) * scale
    return {
        "x_prompt": nrm(ks[0], (BATCH, SEQ, D_MODEL), 1.0),
        "x_sample": nrm(ks[1], (DEC_BATCH, DEC_SEQ, D_MODEL), 1.0),
        "ev_norm": 1.0 + nrm(ks[2], (N_EVEN, D_MODEL), 0.02),
        "ev_w_in": nrm(ks[3], (N_EVEN, D_MODEL, IN_EVEN), D_MODEL ** -0.5),
        "ev_conv_w": nrm(ks[4], (N_EVEN, CONV_K, CONV_WIDTH), CONV_K ** -0.5),
        "ev_conv_b": nrm(ks[5], (N_EVEN, CONV_WIDTH), 0.02),
        "ev_ln_g": 1.0 + nrm(ks[6], (N_EVEN, CONV_WIDTH), 0.02),
        "ev_ln_b": nrm(ks[7], (N_EVEN, CONV_WIDTH), 0.02),
        "ev_w_four": nrm(ks[8], (N_EVEN, FOURIER_GROUPS, FOURIER_GROUP_DIM, FOURIER_GROUP_DIM), FOURIER_GROUP_DIM ** -0.5),
        "ev_w_out": nrm(ks[9], (N_EVEN, MIX_EVEN, D_MODEL), MIX_EVEN ** -0.5),
        "od_norm": 1.0 + nrm(ks[10], (N_ODD, D_MODEL), 0.02),
        "od_w_in": nrm(ks[11], (N_ODD, D_MODEL, IN_ODD), D_MODEL ** -0.5),
        "od_conv_w": nrm(ks[12], (N_ODD, SHORT_K, SHORT_WIDTH), SHORT_K ** -0.5),
        "od_conv_b": nrm(ks[13], (N_ODD, SHORT_WIDTH), 0.02),
        "od_w_out": nrm(ks[14], (N_ODD, SHORT_WIDTH, D_MODEL), SHORT_WIDTH ** -0.5),
        "final_norm": 1.0 + nrm(ks[15], (D_MODEL,), 0.02),
    }


def reference(x_prompt, x_sample, ev_norm, ev_w_in, ev_conv_w, ev_conv_b, ev_ln_g, ev_ln_b,
              ev_w_four, ev_w_out, od_norm, od_w_in, od_conv_w, od_conv_b, od_w_out, final_norm):
    y_prompt = trunk(x_prompt, ev_norm, ev_w_in, ev_conv_w, ev_conv_b, ev_ln_g, ev_ln_b,
                     ev_w_four, ev_w_out, od_norm, od_w_in, od_conv_w, od_conv_b, od_w_out, final_norm)
    y_sample = trunk(x_sample, ev_norm, ev_w_in, ev_conv_w, ev_conv_b, ev_ln_g, ev_ln_b,
                     ev_w_four, ev_w_out, od_norm, od_w_in, od_conv_w, od_conv_b, od_w_out, final_norm)
    return (y_prompt, y_sample)
```

```python
import numpy as np
import ml_dtypes
from contextlib import ExitStack
import concourse.bass as bass
import concourse.mybir as mybir
from concourse.bass_utils import run_bass_kernel_spmd

F32 = mybir.dt.float32
BF = mybir.dt.bfloat16
AF = mybir.ActivationFunctionType
ALU = mybir.AluOpType
D = 1024
KC = 8
T = 512
EPS = 1e-6
NPB = 6
SAME_ENGINE_SYNC = True

P_G1, P_G2, P_B31, P_LNG, P_LNB, P_B3, P_W31, P_W3, P_FLAG = 0, 8, 16, 20, 24, 28, 36, 160, 184
NPRM = 188
FL_SELA, FL_SELB, FL_SELU, FL_LINK = 0, 1, 2, 3


class Sched:
    def __init__(self, nc, stack):
        self.nc = nc
        self.eng = {"pe": nc.tensor, "act": nc.scalar, "dve": nc.vector, "pool": nc.gpsimd, "sp": nc.sync}
        self.esem = {k: stack.enter_context(nc.semaphore("es_" + k)) for k in self.eng}
        self.ecnt = {k: 0 for k in self.eng}
        self.stack = stack
        self.dsem = {}
        self.dcnt = {}
        self.waited = {}
        self.ops = []
        self.lastw = {}
        self.readers = {}

    def add(self, eng, fn, reads=(), writes=(), key=None):
        self.ops.append(dict(eng=eng, fn=fn, reads=list(reads), writes=list(writes), key=key,
                             deps=(), signal=key is not None, sem=None, val=0))

    def _wait(self, e, sem, name, val):
        k = (e, name)
        if self.waited.get(k, 0) < val:
            self.eng[e].wait_ge(sem, val)
            self.waited[k] = val

    def emit(self, barrier=True):
        ops = self.ops
        lastw, readers = self.lastw, self.readers
        for i, op in enumerate(ops):
            deps = set()
            for r in op["reads"]:
                if r in lastw:
                    deps.add(lastw[r])
                if r[0] in ("pb", "pt"):
                    for rd in readers.get(r, ()):
                        if ops[rd]["eng"] != op["eng"]:
                            deps.add(rd)
            for w in op["writes"]:
                if w in lastw:
                    deps.add(lastw[w])
                for rd in readers.get(w, ()):
                    deps.add(rd)
            deps.discard(i)
            best = {}
            for d in deps:
                od = ops[d]
                if od["key"] is None and od["eng"] == op["eng"] and op["key"] is None:
                    if op["eng"] == "pe" or not SAME_ENGINE_SYNC:
                        continue
                gk = ("d", od["key"]) if od["key"] is not None else ("e", od["eng"])
                if best.get(gk, -1) < d:
                    best[gk] = d
            keep = list(best.values())
            for d in keep:
                ops[d]["signal"] = True
            op["deps"] = keep
            for r in op["reads"]:
                readers.setdefault(r, []).append(i)
            for w in op["writes"]:
                lastw[w] = i
                readers[w] = []
        last_of = {}
        for i, op in enumerate(ops):
            if op["key"] is None:
                last_of[op["eng"]] = i
        for i in last_of.values():
            ops[i]["signal"] = True
        for op in ops:
            if not op["signal"]:
                continue
            if op["key"] is not None:
                k = op["key"]
                if k not in self.dsem:
                    self.dsem[k] = self.stack.enter_context(self.nc.semaphore("ds%d" % len(self.dsem)))
                    self.dcnt[k] = 0
                self.dcnt[k] += 1
                op["sem"], op["val"], op["sname"] = self.dsem[k], 16 * self.dcnt[k], ("d", k)
            else:
                e = op["eng"]
                self.ecnt[e] += 1
                op["sem"], op["val"], op["sname"] = self.esem[e], self.ecnt[e], ("e", e)
        for op in ops:
            need = {}
            for d in op["deps"]:
                od = ops[d]
                if need.get(od["sname"], (None, 0))[1] < od["val"]:
                    need[od["sname"]] = (od["sem"], od["val"])
            for name, (sem, val) in need.items():
                self._wait(op["eng"], sem, name, val)
            inst = op["fn"]()
            if op["signal"]:
                inst.then_inc(op["sem"], 16 if op["key"] is not None else 1)
        if barrier:
            for e in self.eng:
                for e2 in self.eng:
                    if e2 != e and self.ecnt[e2] > 0:
                        self._wait(e, self.esem[e2], ("e", e2), self.ecnt[e2])
                for k, sem in self.dsem.items():
                    self._wait(e, sem, ("d", k), 16 * self.dcnt[k])
            self.lastw.clear()
            self.readers.clear()
        else:
            raise NotImplementedError
        self.ops = []


class Builder:
    def __init__(self, ntok_sb, n2_sb, ntok_u, n2_u):
        self.ntok_sb, self.n2_sb, self.ntok_u, self.n2_u = ntok_sb, n2_sb, ntok_u, n2_u
        self.ntok = ntok_sb + ntok_u
        self.nt = self.ntok // T
        self.bankctr = 0
        self.HTN = "ht"

    def bank(self):
        b = self.bankctr % NPB
        self.bankctr += 1
        return b

    def mm(self, out, lhsT, rhs, start, stop, reads, writes):
        nc = self.nc
        self.S.add("pe", lambda: nc.tensor.matmul(out, lhsT, rhs, start=start, stop=stop), reads, writes)

    def act(self, out, in_, func, reads, writes, scale=1.0, bias=0.0, accum_out=None):
        nc = self.nc
        if accum_out is None:
            self.S.add("act", lambda: nc.scalar.activation(out=out, in_=in_, func=func, bias=bias, scale=scale),
                       reads, writes)
        else:
            self.S.add("act", lambda: nc.scalar.activation(out=out, in_=in_, func=func, bias=bias, scale=scale,
                                                           accum_out=accum_out), reads, writes)

    def sqacc(self, in_, acc, reads, writes):
        j = self.jctr % 2
        self.jctr += 1
        self.act(self.JUNK[j][:], in_, AF.Square, reads, list(writes) + [("junk", j)], accum_out=acc)

    def tt(self, eng, out, in0, in1, op, reads, writes):
        e = self.S.eng[eng]
        self.S.add(eng, lambda: e.tensor_tensor(out=out, in0=in0, in1=in1, op=op), reads, writes)

    def ts(self, eng, out, in0, s1, op0, reads, writes, s2=None, op1=None):
        e = self.S.eng[eng]
        if op1 is None:
            self.S.add(eng, lambda: e.tensor_scalar(out=out, in0=in0, scalar1=s1, scalar2=None, op0=op0), reads, writes)
        else:
            self.S.add(eng, lambda: e.tensor_scalar(out=out, in0=in0, scalar1=s1, scalar2=s2, op0=op0, op1=op1),
                       reads, writes)

    def stt(self, out, in0, scalar, in1, op0, op1, reads, writes):
        nc = self.nc
        self.S.add("dve", lambda: nc.vector.scalar_tensor_tensor(out=out, in0=in0, scalar=scalar, in1=in1,
                                                                 op0=op0, op1=op1), reads, writes)

    def cp(self, eng, out, in_, reads, writes):
        if eng == "act":
            self.act(out, in_, AF.Copy, reads, writes)
        else:
            e = self.S.eng[eng]
            self.S.add(eng, lambda: e.tensor_copy(out=out, in_=in_), reads, writes)

    def dma(self, out, in_, reads, writes, key, q="sp"):
        e = self.S.eng[q]
        self.S.add(q, lambda: e.dma_start(out=out, in_=in_), reads, writes, key=key)

    def memset(self, eng, ap, val, writes):
        e = self.S.eng[eng]
        self.S.add(eng, lambda: e.memset(ap, val), [], writes)

    def sb(self, st, name, shape, dt):
        self.uid = getattr(self, "uid", 0) + 1
        return st.enter_context(self.nc.sbuf_tensor("%s_%d" % (name, self.uid), shape, dt))

    def load_weight(self, dst, dst_res, w_ap, col0, ncols, gcol, dst_col0=0, big=False, nslots=8):
        cw = 1024 if big else 512
        for cc in range(ncols // cw):
            for kc in range(KC):
                if big:
                    s = self.wsctr % nslots
                    if s < 4:
                        stg, sres, skey = self.XR[s][:], [("xr", s, 0), ("xr", s, 1)], ("ld", "xrw", s)
                    else:
                        stg, sres, skey = self.XS[:, s - 4, :], [("xsw", s - 4)], ("ld", "xsw", s - 4)
                else:
                    s = self.wsctr % 2
                    stg, sres, skey = self.WS[s][:], [("ws", s)], ("ld", "ws", s)
                self.wsctr += 1
                self.dma(stg, w_ap[kc * 128:(kc + 1) * 128, col0 + cc * cw: col0 + (cc + 1) * cw], [], sres, skey)
                o = dst[:, kc, dst_col0 + cc * cw: dst_col0 + (cc + 1) * cw]
                eng = ("dve", "act", "pool")[self.wsctr % 3]
                rn = [dst_res + (kc, (dst_col0 + cc * cw) // 512 + i) for i in range(cw // 512)]
                if gcol is None:
                    self.cp(eng, o, stg, sres, rn)
                elif eng == "act":
                    self.act(o, stg, AF.Copy, sres, rn, scale=self.PRM[:, gcol + kc: gcol + kc + 1])
                else:
                    self.ts(eng, o, stg, self.PRM[:, gcol + kc: gcol + kc + 1], ALU.mult, sres, rn, s2=0.0, op1=ALU.add)

    def wres(self, name, ncols):
        return [(name, kc, cc) for kc in range(KC) for cc in range(ncols // 512)]

    def norm_sq(self, src, src_res):
        for s in range(4):
            self.sqacc(src[:, s, :], self.SS[:, s:s + 1], [src_res], [("ss", s)])

    def norm_rs(self):
        self.act(self.LNV[:], self.SS[:], AF.Ln, [("ss", s) for s in range(4)], [("lnv",)], scale=1.0 / D, bias=self.EPSC[:, 0:1])
        self.act(self.RSTD[:], self.LNV[:], AF.Exp, [("lnv",)], [("rstd",)], scale=-0.5)

    def norm_stats(self, src, src_res):
        self.norm_sq(src, src_res)
        self.norm_rs()

    def norm_apply_transpose(self, src, src_res, pool_half=False):
        for s in range(4):
            self.norm_apply_sub(src, src_res, s, pool_half)

    def norm_apply_sub(self, src, src_res, s, pool_half=False, part="both"):
        nc = self.nc
        b = s % 2
        if part in ("both", "scale"):
            if pool_half and s % 2 == 1:
                self.ts("pool", self.HTM[b][:], src[:, s, :], self.RSTD[:, s:s + 1], ALU.mult,
                        [src_res, ("rstd",)], [("htm", b)], s2=0.0, op1=ALU.add)
            else:
                self.act(self.HTM[b][:], src[:, s, :], AF.Copy, [src_res, ("rstd",)], [("htm", b)],
                         scale=self.RSTD[:, s:s + 1])
        if part in ("both", "tr"):
            ptv = self.PT[b][:].rearrange("p a (b c) -> p (a b) c", c=128)
            for kc in range(KC):
                o = ptv[:, kc, :]
                i_ = self.HTM[b][:, kc * 128:(kc + 1) * 128]
                self.S.add("pe", lambda o=o, i_=i_: nc.tensor.transpose(o, i_, self.IDENT[:]),
                           [("htm", b)], [("pt", b)])
            self.cp("dve" if (s % 2 == 0 or pool_half) else "act", self.HT[:, :, s * 128:(s + 1) * 128], ptv,
                    [("pt", b)], [(self.HTN, kc) for kc in range(KC)])

    def norm_transpose(self, src, src_res):
        self.norm_stats(src, src_res)
        self.norm_apply_transpose(src, src_res)

    def proj_chunk(self, W, wname, col, b):
        for kc in range(KC):
            self.mm(self.PB[b][:], W[:, kc, col:col + 128], self.HT[:, kc, :], kc == 0, kc == KC - 1,
                    [(self.HTN, kc), (wname, kc, col // 512)], [("pb", b)])

    def res_loads(self, res_d, tok0):
        for s in range(4):
            r0 = tok0 + s * 128
            self.dma(self.XR[s][:], res_d[r0:r0 + 128, :], [], [("xr", s, 0), ("xr", s, 1)], ("ld", "xr", s), q="pool")

    def out_proj(self, WO, wname, res_d, tok0, dst_d, G=None, kc_order=None, preloaded=False):
        order = list(range(KC)) if kc_order is None else kc_order
        used = []
        for s in range(4):
            r = s
            used.append(r)
            XR = self.XR[r]
            r0 = tok0 + s * 128
            if not preloaded:
                self.dma(XR[:], res_d[r0:r0 + 128, :], [], [("xr", r, 0), ("xr", r, 1)], ("ld", "xr", r), q="pool")
            for h in range(2):
                b = self.bank()
                for i, kc in enumerate(order):
                    self.mm(self.PB[b][:], self.MIX[:, kc, s * 128:(s + 1) * 128], WO[:, kc, h * 512:(h + 1) * 512],
                            i == 0, i == KC - 1, [("mix", kc), (wname, kc, h)], [("pb", b)])
                xa = XR[:, h * 512:(h + 1) * 512]
                self.tt("dve", xa, xa, self.PB[b][:], ALU.add, [("pb", b), ("xr", r, h)], [("xr", r, h)])
            if G is None:
                self.dma(dst_d[r0:r0 + 128, :], XR[:], [("xr", r, 0), ("xr", r, 1)], [], ("st", "xr", r), q="pool")
            else:
                self.sqacc(XR[:], self.SS2[:, s:s + 1], [("xr", r, 0), ("xr", r, 1)], [("ss2", s)])
        if G is not None:
            self.act(self.LNV2[:], self.SS2[:], AF.Ln, [("ss2", s) for s in range(4)], [("lnv2",)], scale=1.0 / D,
                     bias=self.EPSC[:, 0:1])
            self.act(self.RSTD2[:], self.LNV2[:], AF.Exp, [("lnv2",)], [("rstd2",)], scale=-0.5)
            for s, r in enumerate(used):
                XR = self.XR[r]
                r0 = tok0 + s * 128
                self.stt(XR[:], XR[:], self.RSTD2[:, s:s + 1], G[:], ALU.mult, ALU.mult,
                         [("xr", r, 0), ("xr", r, 1), ("rstd2",), ("gfin",)], [("xr", r, 0), ("xr", r, 1)])
                self.dma(dst_d[r0:r0 + 128, :], XR[:], [("xr", r, 0), ("xr", r, 1)], [], ("st", "xr", r), q="pool")

    def link_ap(self, b):
        tok = b * T
        if tok == self.ntok_sb // 2:
            return self.PRM[:, P_FLAG + FL_LINK: P_FLAG + FL_LINK + 1]
        if tok == 0 or tok == self.ntok_sb or tok == self.ntok:
            return None
        return self.ONEC[:, 0:1]

    def halos(self, t, CIN, cname, nch, pad):
        cs, ps = t % 2, (t - 1) % 2
        res_c = [(cname, cs, c) for c in range(nch)]
        res_p = [(cname, ps, c) for c in range(nch)]
        lk = self.link_ap(t)
        if t == 0 or lk is None:
            self.memset("pool", CIN[cs][:, :, 0:pad], 0.0, [(cname + "h", cs, "L")])
            if t > 0:
                self.memset("pool", CIN[ps][:, :, pad + T:pad + T + pad], 0.0, [(cname + "h", ps, "R")])
        else:
            self.ts("pool", CIN[ps][:, :, pad + T:pad + T + pad], CIN[cs][:, :, pad:2 * pad], lk, ALU.mult,
                    res_c, [(cname + "h", ps, "R")], s2=0.0, op1=ALU.add)
            self.ts("pool", CIN[cs][:, :, 0:pad], CIN[ps][:, :, T:T + pad], lk, ALU.mult,
                    res_p, [(cname + "h", cs, "L")], s2=0.0, op1=ALU.add)

    def build(self):
        nc = bass.Bass("TRN2", target_bir_lowering=False)
        self.nc = nc
        NTOK = self.ntok
        din = lambda n, shp, dt=F32: nc.dram_tensor(n, shp, dt, kind="ExternalInput").ap()
        x_d = din("x", [NTOK, D])
        prm_d = din("prm", [128, NPRM])
        w_in1 = din("ev_w_in", [D, 2560])
        w_four = din("ev_w_four", [4, 128, 128])
        w_out1 = din("ev_w_out", [D, D])
        w_in2 = din("od_w_in", [D, 4096])
        w_out2 = din("od_w_out", [D, D])
        fin_d = din("final_norm", [D])
        ident_d = din("ident", [128, 128], BF)
        cs_d = din("cs128", [128, 256], BF)
        m1sb_d = din("m1_sb", [128, 256], BF)
        m1u_d = din("m1_u", [128, 256], BF)
        tabsb_d = din("tab_sb", [self.n2_sb, 128, 3 * self.n2_sb], BF)
        tabu_d = din("tab_u", [self.n2_u, 128, 3 * self.n2_u], BF)
        y_d = nc.dram_tensor("y", [NTOK, D], F32, kind="ExternalOutput").ap()
        f_d = nc.dram_tensor("f_scr", [4, 128, NTOK], BF, kind="Internal").ap()
        x1_d = nc.dram_tensor("x1_scr", [NTOK, D], F32, kind="Internal").ap()

        with ExitStack() as top:
            self.S = S = Sched(nc, top)
            sb = self.sb
            self.PRM = sb(top, "prm_sb", [128, NPRM], F32)
            self.IDENT = sb(top, "ident_sb", [128, 128], BF)
            self.ONEC = sb(top, "onec", [128, 1], F32)
            self.EPSC = sb(top, "epsc", [128, 1], F32)
            self.ONEB = sb(top, "oneb", [128, 1], BF)
            self.ONER = sb(top, "oner", [1, 128], F32)
            self.ONEM = sb(top, "onem", [128, 128], BF)
            self.WS = [sb(top, "ws%d" % i, [128, 512], F32) for i in range(2)]
            self.JUNK = [sb(top, "junk%d" % i, [128, D], BF) for i in range(2)]
            self.jctr = 0
            self.SS = sb(top, "ss", [128, 4], F32)
            self.LNV = sb(top, "lnv", [128, 4], F32)
            self.RSTD = sb(top, "rstd", [128, 4], F32)
            self.WM = sb(top, "wm", [128, 4, 6, 128], BF)
            self.PT = [top.enter_context(nc.psum_tensor("pt%d" % i, [128, 2, 512], BF)) for i in range(2)]
            self.PB = [top.enter_context(nc.psum_tensor("pb%d" % i, [128, 512], F32)) for i in range(NPB)]
            self.wsctr = 0

            self.dma(self.PRM[:], prm_d, [], [("prm",)], ("ld", "prm"))
            self.dma(self.IDENT[:], ident_d, [], [("ident",)], ("ld", "ident"))
            self.memset("dve", self.ONEC[:], 1.0, [("onec",)])
            self.memset("dve", self.EPSC[:], EPS, [("epsc",)])
            self.memset("dve", self.ONEB[:], 1.0, [("oneb",)])
            self.memset("dve", self.ONER[:], 1.0, [("oner",)])
            self.memset("dve", self.ONEM[:], 1.0, [("onem",)])
            with ExitStack() as st:
                CS = sb(st, "cs_sb", [128, 256], BF)
                WF4s = sb(st, "wf4s", [128, 4, 128], F32)
                WF4 = sb(st, "wf4", [128, 4, 128], BF)
                self.dma(CS[:], cs_d, [], [("cs",)], ("ld", "cs"))
                self.dma(WF4s[:], w_four.rearrange("g d e -> d g e"), [], [("wf4s",)], ("ld", "wf4s"))
                self.cp("dve", WF4[:], WF4s[:], [("wf4s",)], [("wf4",)])
                for g in range(4):
                    b = self.bank()
                    self.mm(self.PB[b][:, 0:128], CS[:, 0:128], WF4[:, g, :], True, True, [("cs",), ("wf4",)], [("pb", b)])
                    self.mm(self.PB[b][:, 128:256], CS[:, 128:256], WF4[:, g, :], True, True, [("cs",), ("wf4",)], [("pb", b)])
                    for v, fl in enumerate((FL_SELA, FL_SELB, FL_SELU)):
                        for cs_ in range(2):
                            self.ts("dve", self.WM[:, g, 2 * v + cs_, :], self.PB[b][:, cs_ * 128:(cs_ + 1) * 128],
                                    self.PRM[:, P_FLAG + fl:P_FLAG + fl + 1], ALU.mult,
                                    [("pb", b), ("prm",)], [("wm", g, 2 * v + cs_)])
                S.emit()

            with ExitStack() as stf:
                self.WF = self.sb(stf, "wf", [128, KC, 512], BF)
                self.load_weight(self.WF, ("wf",), w_in1, 1024, 512, P_G1)
                self.fourier_block(0, self.ntok_sb, self.n2_sb, True, m1sb_d, tabsb_d, x_d, w_in1, f_d)
                self.fourier_block(self.ntok_sb, self.ntok_u, self.n2_u, False, m1u_d, tabu_d, x_d, w_in1, f_d)

            self.layer1(x_d, w_in1, w_out1, f_d, x1_d)
            self.layer2(x1_d, w_in2, w_out2, fin_d, y_d)
        return nc

    def fourier_block(self, tb, ntok, N2, dual, m1_d, tab_d, x_d, w_in1, f_d):
        nc, S, sb = self.nc, self.S, self.sb
        with ExitStack() as st:
            Fb = sb(st, "Fb", [128, 4, N2, 128], BF)
            with ExitStack() as st1:
                WF = self.WF
                XS2 = [sb(st1, "xsf%d" % i, [128, 4, D], F32) for i in range(2)]
                self.HTM = [sb(st1, "htmf%d" % i, [128, D], BF) for i in range(2)]
                HT2 = [sb(st1, "htf%d" % i, [128, KC, T], BF) for i in range(2)]
                wfres = self.wres("wf", 512) if tb == 0 else []
                xv = x_d[tb:tb + ntok, :].rearrange("(p n) d -> p n d", n=N2)
                NG = N2 // 4
                self.dma(XS2[0][:], xv[:, 0:4, :], [], [("xs", 0)], ("ld", "xs", 0))
                for q in range(NG):
                    xq = q % 2
                    if q + 1 < NG:
                        self.dma(XS2[1 - xq][:], xv[:, 4 * q + 4:4 * q + 8, :], [], [("xs", 1 - xq)], ("ld", "xs", 1 - xq))
                    self.HT, self.HTN = HT2[xq], "htf%d" % xq
                    self.norm_stats(XS2[xq], ("xs", xq))
                    self.norm_apply_transpose(XS2[xq], ("xs", xq), pool_half=True)
                    for i in range(4):
                        n2 = 4 * q + i
                        b = self.bank()
                        for kc in range(KC):
                            self.mm(self.PB[b][:], self.HT[:, kc, i * 128:(i + 1) * 128], WF[:, kc, :], kc == 0, kc == KC - 1,
                                    [(self.HTN, kc)] + wfres, [("pb", b)])
                        self.cp("dve", Fb[:, :, n2, :], self.PB[b][:].rearrange("p (g c) -> p g c", c=128),
                                [("pb", b)], [("F", n2)])
                S.emit()
                self.HTN = "ht"
            Fres = [("F", n2) for n2 in range(N2)]
            A = sb(st, "Adft", [N2, 128, 256], BF)
            Yr = sb(st, "Yr", [128, N2, 128], BF)
            Yi = sb(st, "Yi", [128, N2, 128], BF)
            M1 = sb(st, "m1", [128, 256], BF)
            JC = 16
            TAB = [sb(st, "tab%d" % i, [N2, JC, 3 * N2], BF) for i in range(2)]
            FST = [sb(st, "fst%d" % i, [128, 512], BF) for i in range(2)]
            self.dma(M1[:], m1_d, [], [("m1",)], ("ld", "m1"))
            JB = min(512 // (2 * N2), JC)
            tabctr = 0
            evc = 0
            for g in range(4):
                for cp_ in range(64):
                    b = self.bank()
                    for k in range(2):
                        c = 2 * cp_ + k
                        self.mm(self.PB[b][0:N2, k * 256:(k + 1) * 256], Fb[:, g, :, c], M1[:], True, True,
                                Fres + [("m1",)], [("pb", b)])
                    evc += 1
                    self.cp("act" if evc % 2 else "dve", A[:, 2 * cp_:2 * cp_ + 2, :],
                            self.PB[b][0:N2, :].rearrange("p (k j) -> p k j", k=2), [("pb", b)], [("A", cp_)])
                Ares = [("A", i) for i in range(64)]
                for jc in range(128 // JC):
                    tsl = tabctr % 2
                    tabctr += 1
                    self.dma(TAB[tsl][:], tab_d[:, jc * JC:(jc + 1) * JC, :], [], [("tab", tsl)], ("ld", "tab", tsl))
                    for jb in range(JC // JB):
                        b = self.bank()
                        for jq in range(JB):
                            jj = jb * JB + jq
                            j = jc * JC + jj
                            o = self.PB[b][:, jq * 2 * N2:(jq + 1) * 2 * N2]
                            self.mm(o, A[:, :, j], TAB[tsl][:, jj, N2:3 * N2], True, False, Ares + [("tab", tsl)], [("pb", b)])
                            self.mm(o, A[:, :, 128 + j], TAB[tsl][:, jj, 0:2 * N2], False, True, Ares + [("tab", tsl)], [("pb", b)])
                        j0 = jc * JC + jb * JB
                        pv = self.PB[b][:, 0:JB * 2 * N2].rearrange("p (j r m) -> p r m j", r=2, m=N2)
                        evc += 1
                        self.cp("act" if evc % 2 else "dve", Yr[:, :, j0:j0 + JB], pv[:, 0], [("pb", b)], [("Y", 0, j0)])
                        self.cp("act" if evc % 2 else "dve", Yi[:, :, j0:j0 + JB], pv[:, 1], [("pb", b)], [("Y", 1, j0)])
                Yres = [("Y", r, j0) for r in range(2) for j0 in range(0, 128, JB)]
                rowlen = N2 * 128
                for pt_ in range(ntok // 512):
                    P0 = pt_ * 512
                    b = self.bank()
                    if dual:
                        Sh = ntok // 2
                        hp = P0 // Sh
                        q = (P0 - Sh * hp) // 512
                        offB = 128 * 8 * q + 64 * hp
                        apB = [[rowlen, 128], [128, 8], [1, 64]]
                        ops_ = [(0, Yr[:, :, :].rearrange("p m j -> p (m j)")[:, P0:P0 + 512]),
                                (1, Yi[:, :, :].rearrange("p m j -> p (m j)")[:, P0:P0 + 512]),
                                (2, bass.AP(Yr, offB, apB)), (3, bass.AP(Yi, offB, apB))]
                    else:
                        ops_ = [(4, Yr[:, :, :].rearrange("p m j -> p (m j)")[:, P0:P0 + 512]),
                                (5, Yi[:, :, :].rearrange("p m j -> p (m j)")[:, P0:P0 + 512])]
                    for i, (v, rhs) in enumerate(ops_):
                        self.mm(self.PB[b][:], self.WM[:, g, v, :], rhs, i == 0, i == len(ops_) - 1,
                                Yres + [("wm", g, v)], [("pb", b)])
                    fs = pt_ % 2
                    evc += 1
                    self.cp("act" if evc % 2 else "dve", FST[fs][:], self.PB[b][:], [("pb", b)], [("fst", fs)])
                    self.dma(f_d[g, :, tb + P0: tb + P0 + 512], FST[fs][:], [("fst", fs)], [], ("st", "fst", fs))
            S.emit()

    def layer1(self, x_d, w_in1, w_out1, f_d, x1_d):
        nc, S, sb = self.nc, self.S, self.sb
        NT = self.nt
        with ExitStack() as st:
            W2 = sb(st, "w2", [128, KC, 2048], BF)
            WO = sb(st, "wo1", [128, KC, D], BF)
            D31 = sb(st, "d31", [128, 4, 31, 128], BF)
            self.XS = sb(st, "xs", [128, 4, D], F32)
            self.XR = [sb(st, "xr%d" % i, [128, D], F32) for i in range(4)]
            self.xrctr = 0
            self.HTM = [sb(st, "htm%d" % i, [128, D], BF) for i in range(2)]
            self.HT = sb(st, "ht", [128, KC, T], BF)
            CIN = [sb(st, "cin%d" % i, [128, 4, T + 30], BF) for i in range(2)]
            SZ = [sb(st, "sz%d" % i, [128, 8, T], BF) for i in range(2)]
            SIG = [sb(st, "sig%d" % i, [128, T], BF) for i in range(2)]
            CO = sb(st, "co", [128, 4, T], F32)
            COB = [sb(st, "cob%d" % i, [128, T], BF) for i in range(2)]
            SQ = [sb(st, "sq%d" % i, [128, T], BF) for i in range(2)]
            ROW = sb(st, "row", [128, T], F32)
            ROW2 = sb(st, "row2", [128, T], F32)
            LT = [sb(st, "lt%d" % i, [128, T], F32) for i in range(2)]
            LS = [sb(st, "ls%d" % i, [128, T], BF) for i in range(2)]
            self.MIX = sb(st, "mix", [128, 8, T], BF)
            FSL = sb(st, "fsl", [128, 4, T], BF)
            PRM = self.PRM
            self.dma(self.XS[:], x_d[0:T, :].rearrange("(s p) d -> p s d", p=128), [],
                     [("xs",)] + [("xsw", i) for i in range(4)], ("ld", "xs"))
            self.load_weight(W2, ("w2",), w_in1, 0, 1024, P_G1, 0, big=True, nslots=4)
            def build_d31():
                for c in range(4):
                    ident_b = bass.AP(self.IDENT, 0, [[128, 128], [0, 31], [1, 128]])
                    w_b = bass.AP(self.PRM, P_W31 + c * 31, [[NPRM, 128], [1, 31], [0, 128]])
                    self.stt(D31[:, c, :, :], ident_b, 0.5, w_b, ALU.mult, ALU.mult, [], [("d31", c)])

            for i in range(2):
                self.memset("pool", CIN[i][:], 0.0, [("cin", i, c) for c in range(4)] + [("cinh", i, "L"), ("cinh", i, "R")])
            st2 = {}

            def xload(t):
                tok0 = t * T
                self.dma(self.XS[:], x_d[tok0:tok0 + T, :].rearrange("(s p) d -> p s d", p=128), [],
                         [("xs",)] + [("xsw", i) for i in range(4)], ("ld", "xs"))

            def stage1a(t):
                cs = t % 2
                for c in range(4):
                    bg = self.bank()
                    self.proj_chunk(W2, "w2", 512 + c * 128, bg)
                    self.act(SIG[c % 2][:], self.PB[bg][:], AF.Tanh, [("pb", bg)], [("sig", c % 2)], scale=0.5)
                    bv = self.bank()
                    self.proj_chunk(W2, "w2", c * 128, bv)
                    self.stt(CIN[cs][:, c, 15:15 + T], SIG[c % 2][:], 1.0, self.PB[bv][:], ALU.add, ALU.mult,
                             [("pb", bv), ("sig", c % 2)], [("cin", cs, c)])

            def stage1z(t, lo, hi):
                cs = t % 2
                for c in range(lo, hi):
                    bz = self.bank()
                    self.proj_chunk(W2, "w2", 1024 + c * 128, bz)
                    self.act(SZ[cs][:, c, :], self.PB[bz][:], AF.Silu, [("pb", bz)], [("sz", cs, c)])

            def stage2a(t):
                cs = t % 2
                tok0 = t * T
                self.dma(FSL[:], f_d[:, :, tok0:tok0 + T].rearrange("g p n -> p g n"), [], [("fsl",)], ("ld", "fsl"))
                cb = [self.bank() for _ in range(4)]
                bm = self.bank()
                bq = self.bank()

                def stats(c):
                    k = c % 2
                    self.mm(self.PB[bm][:], self.ONEM[:], COB[k][:], c == 0, c == 3, [("cob", k), ("onem",)], [("pb", bm)])
                    self.mm(self.PB[bq][:], self.ONEM[:], SQ[k][:], c == 0, c == 3, [("sq", k), ("onem",)], [("pb", bq)])

                for c in range(4):
                    b = cb[c]
                    k = c % 2
                    for tap in range(31):
                        self.mm(self.PB[b][:], D31[:, c, tap, :], CIN[cs][:, c, tap:tap + T], tap == 0, tap == 30,
                                [("cin", cs, c), ("cinh", cs, "L"), ("cinh", cs, "R"), ("d31", c)], [("pb", b)])
                    if c >= 1:
                        stats(c - 1)
                    bia = PRM[:, P_B31 + c:P_B31 + c + 1]
                    self.act(CO[:, c, :], self.PB[b][:], AF.Identity, [("pb", b)], [("co", c)], bias=bia)
                    self.act(COB[k][:], self.PB[b][:], AF.Identity, [("pb", b)], [("cob", k)], bias=bia)
                    self.act(SQ[k][:], self.PB[b][:], AF.Square, [("pb", b)], [("sq", k)], bias=bia)
                st2["stats3"] = lambda: stats(3)
                st2["bm"], st2["bq"] = bm, bq

            def stage2fin(t):
                st2["stats3"]()
                bm, bq = st2["bm"], st2["bq"]
                self.act(ROW[:], self.PB[bm][:], AF.Square, [("pb", bm)], [("row", 0)], scale=1.0 / 512)
                self.stt(ROW[:], self.PB[bq][:], 1.0 / 512, ROW[:], ALU.mult, ALU.subtract, [("pb", bq), ("row", 0)], [("row", 0)])
                if st2.get("pending_rs"):
                    self.norm_rs()
                    st2["pending_rs"] = False
                self.act(ROW[:], ROW[:], AF.Ln, [("row", 0)], [("row", 0)], bias=self.EPSC[:, 0:1])
                self.act(ROW[:], ROW[:], AF.Exp, [("row", 0)], [("row", 0)], scale=-0.5)
                self.stt(ROW2[:], self.PB[bm][:], -1.0 / 512, ROW[:], ALU.mult, ALU.mult, [("pb", bm), ("row", 0)], [("row", 1)])

            def stage2bc(t):
                pass

            def stage2ln(t, c0, c1):
                cs = t % 2
                for c in range(c0, c1):
                    k = c % 2
                    self.tt("dve", LT[k][:], CO[:, c, :], ROW[:], ALU.mult, [("co", c), ("row", 0)], [("lt", k)])
                    self.tt("dve", LT[k][:], LT[k][:], ROW2[:], ALU.add, [("lt", k), ("row", 1)], [("lt", k)])
                    self.act(LS[k][:], LT[k][:], AF.Silu, [("lt", k)], [("ls", k)],
                             scale=PRM[:, P_LNG + c:P_LNG + c + 1], bias=PRM[:, P_LNB + c:P_LNB + c + 1])
                    self.tt("dve", self.MIX[:, c, :], LS[k][:], SZ[cs][:, c, :], ALU.mult,
                            [("ls", k), ("sz", cs, c)], [("mix", c)])

            def stage2gate(t):
                cs = t % 2
                for g in range(4):
                    self.tt("pool", self.MIX[:, 4 + g, :], FSL[:, g, :], SZ[cs][:, 4 + g, :], ALU.mult,
                            [("fsl",), ("sz", cs, 4 + g)], [("mix", 4 + g)])

            def stage2op(t):
                self.out_proj(WO, "wo", x_d, t * T, x1_d, preloaded=True)

            self.norm_stats(self.XS, ("xs",))
            self.norm_apply_transpose(self.XS, ("xs",))
            if NT > 1:
                xload(1)
            for t in range(NT + 2):
                if t < NT:
                    stage1a(t)
                    self.halos(t, CIN, "cin", 4, 15)
                elif t == NT:
                    self.memset("pool", CIN[(t - 1) % 2][:, :, 15 + T:30 + T], 0.0, [("cinh", (t - 1) % 2, "R")])
                if t == 0:
                    build_d31()
                    self.load_weight(W2, ("w2",), w_in1, 1536, 1024, P_G1, 1024, big=True, nslots=4)
                if t >= 2:
                    stage2op(t - 2)
                if t < NT:
                    stage1z(t, 0, 4)
                if t == 0:
                    self.load_weight(WO, ("wo",), w_out1, 0, D, None, big=True, nslots=4)
                s2 = 1 <= t <= NT
                if t + 1 < NT:
                    self.norm_sq(self.XS, ("xs",))
                    st2["pending_rs"] = True
                if s2:
                    stage2a(t - 1)
                    self.res_loads(x_d, (t - 1) * T)
                if t < NT:
                    stage1z(t, 4, 5)
                if s2:
                    stage2fin(t - 1)
                if st2.get("pending_rs"):
                    self.norm_rs()
                    st2["pending_rs"] = False
                if t + 1 < NT:
                    self.norm_apply_sub(self.XS, ("xs",), 0, True, "scale")
                    self.norm_apply_sub(self.XS, ("xs",), 1, True, "scale")
                if t < NT:
                    stage1z(t, 5, 8)
                if t + 1 < NT:
                    self.norm_apply_sub(self.XS, ("xs",), 0, True, "tr")
                    self.norm_apply_sub(self.XS, ("xs",), 1, True, "tr")
                    self.norm_apply_sub(self.XS, ("xs",), 2, True)
                    self.norm_apply_sub(self.XS, ("xs",), 3, True)
                    if t + 2 < NT:
                        xload(t + 2)
                if s2:
                    stage2bc(t - 1)
                    stage2gate(t - 1)
                    stage2ln(t - 1, 0, 4)
            S.emit()

    def layer2(self, x1_d, w_in2, w_out2, fin_d, y_d):
        nc, S, sb = self.nc, self.S, self.sb
        NT = self.nt
        with ExitStack() as st:
            W3 = sb(st, "w3", [128, KC, 4096], BF)
            WO = sb(st, "wo2", [128, KC, D], BF)
            D3 = sb(st, "d3", [128, 8, 3, 128], BF)
            G = sb(st, "gfin", [128, D], F32)
            self.XS = sb(st, "xsb", [128, 4, D], F32)
            self.XR = [sb(st, "xrb%d" % i, [128, D], F32) for i in range(4)]
            self.xrctr = 0
            self.SS2 = sb(st, "ss2", [128, 4], F32)
            self.LNV2 = sb(st, "lnv2", [128, 4], F32)
            self.RSTD2 = sb(st, "rstd2", [128, 4], F32)
            self.HTM = [sb(st, "htmb%d" % i, [128, D], BF) for i in range(2)]
            self.HT = sb(st, "htb", [128, KC, T], BF)
            CIN = [sb(st, "cu%d" % i, [128, 8, T + 2], BF) for i in range(2)]
            BGZ = [sb(st, "bgz%d" % i, [128, 8, T], BF) for i in range(2)]
            UT = [sb(st, "ut%d" % i, [128, T], BF) for i in range(2)]
            ZT = [sb(st, "zt%d" % i, [128, T], BF) for i in range(2)]
            self.MIX = sb(st, "mixb", [128, 8, T], BF)
            PRM = self.PRM
            self.dma(self.XS[:], x1_d[0:T, :].rearrange("(s p) d -> p s d", p=128), [],
                     [("xs",)] + [("xsw", i) for i in range(4)], ("ld", "xs"))
            for c0 in (0, 2048):
                self.load_weight(W3, ("w3",), w_in2, c0, 1024, P_G2, c0, big=True, nslots=4)
            self.dma(G[:], fin_d.partition_broadcast(128), [], [("gfin",)], ("ld", "gfin"))
            def build_d3():
                ident_b = bass.AP(self.IDENT, 0, [[128, 128], [0, 24], [1, 128]])
                w_b = bass.AP(self.PRM, P_W3, [[NPRM, 128], [1, 24], [0, 128]])
                self.stt(D3[:, :, :, :].rearrange("p c t m -> p (c t) m"), ident_b, 1.0, w_b, ALU.mult, ALU.mult,
                         [], [("d3", c) for c in range(8)])

            for i in range(2):
                self.memset("pool", CIN[i][:], 0.0, [("cu", i, c) for c in range(8)] + [("cuh", i, "L"), ("cuh", i, "R")])

            def xload(t):
                tok0 = t * T
                self.dma(self.XS[:], x1_d[tok0:tok0 + T, :].rearrange("(s p) d -> p s d", p=128), [],
                         [("xs",)] + [("xsw", i) for i in range(4)], ("ld", "xs"))

            def stage1a(t):
                cs = t % 2
                for c in range(8):
                    k = c % 2
                    bu = self.bank()
                    self.proj_chunk(W3, "w3", c * 128, bu)
                    self.act(UT[k][:], self.PB[bu][:], AF.Copy, [("pb", bu)], [("ut", k)])
                    bc = self.bank()
                    self.proj_chunk(W3, "w3", 2048 + c * 128, bc)
                    self.tt("dve", CIN[cs][:, c, 1:1 + T], self.PB[bc][:], UT[k][:], ALU.mult,
                            [("pb", bc), ("ut", k)], [("cu", cs, c)])

            def stage1z(t):
                cs = t % 2
                for c in range(8):
                    k = c % 2
                    bz = self.bank()
                    self.proj_chunk(W3, "w3", 3072 + c * 128, bz)
                    self.act(ZT[k][:], self.PB[bz][:], AF.Silu, [("pb", bz)], [("zt", k)])
                    bb = self.bank()
                    self.proj_chunk(W3, "w3", 1024 + c * 128, bb)
                    self.tt("dve", BGZ[cs][:, c, :], self.PB[bb][:], ZT[k][:], ALU.mult,
                            [("pb", bb), ("zt", k)], [("bgz", cs, c)])

            def stage2c(t):
                cs = t % 2
                for c in range(8):
                    b = self.bank()
                    for tap in range(3):
                        self.mm(self.PB[b][:], D3[:, c, tap, :], CIN[cs][:, c, tap:tap + T], tap == 0, tap == 2,
                                [("cu", cs, c), ("cuh", cs, "L"), ("cuh", cs, "R"), ("d3", c)], [("pb", b)])
                    self.stt(self.MIX[:, c, :], self.PB[b][:], PRM[:, P_B3 + c:P_B3 + c + 1], BGZ[cs][:, c, :],
                             ALU.add, ALU.mult, [("pb", b), ("bgz", cs, c)], [("mix", c)])

            def stage2o(t):
                tok0 = t * T
                self.out_proj(WO, "wo", x1_d, tok0, y_d, G=G)

            self.norm_stats(self.XS, ("xs",))
            self.norm_apply_transpose(self.XS, ("xs",))
            if NT > 1:
                xload(1)
            for t in range(NT + 1):
                if t < NT:
                    stage1a(t)
                    self.halos(t, CIN, "cu", 8, 1)
                else:
                    self.memset("pool", CIN[(t - 1) % 2][:, :, 1 + T:2 + T], 0.0, [("cuh", (t - 1) % 2, "R")])
                if t == 0:
                    build_d3()
                    for c0 in (3072, 1024):
                        self.load_weight(W3, ("w3",), w_in2, c0, 1024, P_G2, c0, big=True, nslots=4)
                if t + 1 < NT:
                    self.norm_stats(self.XS, ("xs",))
                if t >= 1:
                    stage2c(t - 1)
                if t < NT:
                    stage1z(t)
                if t == 0:
                    self.load_weight(WO, ("wo",), w_out2, 0, D, None, big=True, nslots=4)
                if t + 1 < NT:
                    self.norm_apply_transpose(self.XS, ("xs",), pool_half=True)
                    if t + 2 < NT:
                        xload(t + 2)
                if t >= 1:
                    stage2o(t - 1)
            S.emit()


def dft_tables(N2, mode):
    n1 = np.arange(128)[:, None]
    j = np.arange(128)[None, :]
    n2 = np.arange(N2)[:, None, None]
    jj = np.arange(128)[None, :, None]
    m = np.arange(N2)[None, None, :]
    if mode == "A":
        ang = 2 * np.pi * n1 * j / 128
        M1 = np.concatenate([np.cos(ang), -np.sin(ang)], 1)
        th = 2 * np.pi * n2 * (jj + 128 * m) / (128 * N2)
    else:
        mask = ((n1 // 64) == (j // 64))
        ang = 2 * np.pi * (n1 % 64) * (j % 64) / 64
        M1 = np.concatenate([np.cos(ang) * mask, -np.sin(ang) * mask], 1)
        th = 2 * np.pi * n2 * ((jj % 64) + 64 * m) / (64 * N2)
    Wr, Wi = np.cos(th), -np.sin(th)
    Tab = np.concatenate([-Wi, Wr, Wi], 2)
    return M1.astype(ml_dtypes.bfloat16), np.ascontiguousarray(Tab).astype(ml_dtypes.bfloat16)


def pack_prm(inp, mode, ntok_sb, ntok_u):
    prm = np.zeros((128, NPRM), np.float32)
    col = lambda v: np.ascontiguousarray(np.asarray(v, np.float32).reshape(-1, 128).T)
    prm[:, P_G1:P_G1 + 8] = col(inp["ev_norm"][0])
    prm[:, P_G2:P_G2 + 8] = col(inp["od_norm"][0])
    prm[:, P_B31:P_B31 + 4] = col(inp["ev_conv_b"][0])
    prm[:, P_LNG:P_LNG + 4] = col(inp["ev_ln_g"][0])
    prm[:, P_LNB:P_LNB + 4] = col(inp["ev_ln_b"][0])
    prm[:, P_B3:P_B3 + 8] = col(inp["od_conv_b"][0])
    w31 = np.asarray(inp["ev_conv_w"][0], np.float32)
    prm[:, P_W31:P_W31 + 124] = w31.T.reshape(4, 128, 31).transpose(1, 0, 2).reshape(128, 124)
    w3 = np.asarray(inp["od_conv_w"][0], np.float32)
    prm[:, P_W3:P_W3 + 24] = w3.T.reshape(8, 128, 3).transpose(1, 0, 2).reshape(128, 24)
    prm[:, P_FLAG + FL_SELA] = (1.0 / np.sqrt(128.0 * ntok_sb)) if mode == "A" else 0.0
    prm[:, P_FLAG + FL_SELB] = (1.0 / np.sqrt(128.0 * ntok_sb / 2)) if mode == "B" else 0.0
    prm[:, P_FLAG + FL_SELU] = 1.0 / np.sqrt(128.0 * ntok_u)
    prm[:, P_FLAG + FL_LINK] = 1.0 if mode == "A" else 0.0
    return prm


_NC_CACHE = {}


def run_cores(core_x, core_modes, inp, ntok_sb, n2_sb, ntok_u, n2_u):
    key = (ntok_sb, n2_sb, ntok_u, n2_u)
    if key not in _NC_CACHE:
        _NC_CACHE[key] = Builder(ntok_sb, n2_sb, ntok_u, n2_u).build()
    nc = _NC_CACHE[key]
    c = np.arange(128)
    ang = 2 * np.pi * np.outer(c, c) / 128
    cs128 = np.concatenate([np.cos(ang), np.sin(ang)], 1).astype(ml_dtypes.bfloat16)
    ident = np.eye(128, dtype=np.float32).astype(ml_dtypes.bfloat16)
    tabs = {m: dft_tables(n2_sb, m) for m in set(core_modes)}
    m1u, tabu = dft_tables(n2_u, "A")
    f32 = lambda a: np.ascontiguousarray(np.asarray(a, np.float32))
    in_maps = []
    for xc, mode in zip(core_x, core_modes):
        in_maps.append({
            "x": f32(xc), "prm": pack_prm(inp, mode, ntok_sb, ntok_u),
            "ev_w_in": f32(inp["ev_w_in"][0]), "ev_w_four": f32(inp["ev_w_four"][0]),
            "ev_w_out": f32(inp["ev_w_out"][0]), "od_w_in": f32(inp["od_w_in"][0]),
            "od_w_out": f32(inp["od_w_out"][0]), "final_norm": f32(inp["final_norm"]),
            "ident": ident, "cs128": cs128, "m1_sb": tabs[mode][0], "tab_sb": tabs[mode][1],
            "m1_u": m1u, "tab_u": tabu,
        })
    res = run_bass_kernel_spmd(nc, in_maps, core_ids=list(range(len(in_maps))))
    return [np.asarray(r["y"], np.float32) for r in res.results]


def kernel(x_prompt, x_sample, **w):
    xp = np.asarray(x_prompt, np.float32)
    xs = np.asarray(x_sample, np.float32)
    core_x, modes, plan = [], [], []
    for i in range(4):
        core_x.append(np.concatenate([xs[i], xp[i]], 0))
        modes.append("A")
        plan.append([("s", i), ("p", i)])
    for i in range(4):
        ids = [4 + 3 * i, 5 + 3 * i, 6 + 3 * i]
        core_x.append(np.concatenate([xp[k] for k in ids], 0))
        modes.append("B")
        plan.append([("p", k) for k in ids])
    outs = run_cores(core_x, modes, w, 8192, 64, 4096, 32)
    yp = np.empty_like(xp)
    ys = np.empty_like(xs)
    for o, pl in zip(outs, plan):
        off = 0
        for kind, k in pl:
            if kind == "s":
                ys[k] = o[off:off + 8192]
                off += 8192
            else:
                yp[k] = o[off:off + 4096]
                off += 4096
    return (yp, ys)
```

```python
import numpy as np
import ml_dtypes
from contextlib import ExitStack
import concourse.bass as bass
import concourse.mybir as mybir
from concourse.bass_utils import run_bass_kernel_spmd

F32 = mybir.dt.float32
BF = mybir.dt.bfloat16
AF = mybir.ActivationFunctionType
ALU = mybir.AluOpType
D = 1024
KC = 8
T = 512
EPS = 1e-6
NPB = 6
SAME_ENGINE_SYNC = True

P_G1, P_G2, P_B31, P_LNG, P_LNB, P_B3, P_W31, P_W3, P_FLAG = 0, 8, 16, 20, 24, 28, 36, 160, 184
NPRM = 188
FL_SELA, FL_SELB, FL_SELU, FL_LINK = 0, 1, 2, 3


class Sched:
    def __init__(self, nc, stack):
        self.nc = nc
        self.eng = {"pe": nc.tensor, "act": nc.scalar, "dve": nc.vector, "pool": nc.gpsimd, "sp": nc.sync}
        self.esem = {k: stack.enter_context(nc.semaphore("es_" + k)) for k in self.eng}
        self.ecnt = {k: 0 for k in self.eng}
        self.stack = stack
        self.dsem = {}
        self.dcnt = {}
        self.waited = {}
        self.ops = []
        self.lastw = {}
        self.readers = {}

    def add(self, eng, fn, reads=(), writes=(), key=None):
        self.ops.append(dict(eng=eng, fn=fn, reads=list(reads), writes=list(writes), key=key,
                             deps=(), signal=key is not None, sem=None, val=0))

    def _wait(self, e, sem, name, val):
        k = (e, name)
        if self.waited.get(k, 0) < val:
            self.eng[e].wait_ge(sem, val)
            self.waited[k] = val

    def emit(self, barrier=True):
        ops = self.ops
        lastw, readers = self.lastw, self.readers
        for i, op in enumerate(ops):
            deps = set()
            for r in op["reads"]:
                if r in lastw:
                    deps.add(lastw[r])
                if r[0] in ("pb", "pt"):
                    for rd in readers.get(r, ()):
                        if ops[rd]["eng"] != op["eng"]:
                            deps.add(rd)
            for w in op["writes"]:
                if w in lastw:
                    deps.add(lastw[w])
                for rd in readers.get(w, ()):
                    deps.add(rd)
            deps.discard(i)
            best = {}
            for d in deps:
                od = ops[d]
                if od["key"] is None and od["eng"] == op["eng"] and op["key"] is None:
                    if op["eng"] == "pe" or not SAME_ENGINE_SYNC:
                        continue
                gk = ("d", od["key"]) if od["key"] is not None else ("e", od["eng"])
                if best.get(gk, -1) < d:
                    best[gk] = d
            keep = list(best.values())
            for d in keep:
                ops[d]["signal"] = True
            op["deps"] = keep
            for r in op["reads"]:
                readers.setdefault(r, []).append(i)
            for w in op["writes"]:
                lastw[w] = i
                readers[w] = []
        last_of = {}
        for i, op in enumerate(ops):
            if op["key"] is None:
                last_of[op["eng"]] = i
        for i in last_of.values():
            ops[i]["signal"] = True
        for op in ops:
            if not op["signal"]:
                continue
            if op["key"] is not None:
                k = op["key"]
                if k not in self.dsem:
                    self.dsem[k] = self.stack.enter_context(self.nc.semaphore("ds%d" % len(self.dsem)))
                    self.dcnt[k] = 0
                self.dcnt[k] += 1
                op["sem"], op["val"], op["sname"] = self.dsem[k], 16 * self.dcnt[k], ("d", k)
            else:
                e = op["eng"]
                self.ecnt[e] += 1
                op["sem"], op["val"], op["sname"] = self.esem[e], self.ecnt[e], ("e", e)
        for op in ops:
            need = {}
            for d in op["deps"]:
                od = ops[d]
                if need.get(od["sname"], (None, 0))[1] < od["val"]:
                    need[od["sname"]] = (od["sem"], od["val"])
            for name, (sem, val) in need.items():
                self._wait(op["eng"], sem, name, val)
            inst = op["fn"]()
            if op["signal"]:
                inst.then_inc(op["sem"], 16 if op["key"] is not None else 1)
        if barrier:
            for e in self.eng:
                for e2 in self.eng:
                    if e2 != e and self.ecnt[e2] > 0:
                        self._wait(e, self.esem[e2], ("e", e2), self.ecnt[e2])
                for k, sem in self.dsem.items():
                    self._wait(e, sem, ("d", k), 16 * self.dcnt[k])
            self.lastw.clear()
            self.readers.clear()
        else:
            raise NotImplementedError
        self.ops = []


class Builder:
    def __init__(self, ntok_sb, n2_sb, ntok_u, n2_u):
        self.ntok_sb, self.n2_sb, self.ntok_u, self.n2_u = ntok_sb, n2_sb, ntok_u, n2_u
        self.ntok = ntok_sb + ntok_u
        self.nt = self.ntok // T
        self.bankctr = 0
        self.HTN = "ht"

    def bank(self):
        b = self.bankctr % NPB
        self.bankctr += 1
        return b

    def mm(self, out, lhsT, rhs, start, stop, reads, writes):
        nc = self.nc
        self.S.add("pe", lambda: nc.tensor.matmul(out, lhsT, rhs, start=start, stop=stop), reads, writes)

    def act(self, out, in_, func, reads, writes, scale=1.0, bias=0.0, accum_out=None):
        nc = self.nc
        if accum_out is None:
            self.S.add("act", lambda: nc.scalar.activation(out=out, in_=in_, func=func, bias=bias, scale=scale),
                       reads, writes)
        else:
            self.S.add("act", lambda: nc.scalar.activation(out=out, in_=in_, func=func, bias=bias, scale=scale,
                                                           accum_out=accum_out), reads, writes)

    def sqacc(self, in_, acc, reads, writes):
        j = self.jctr % 2
        self.jctr += 1
        self.act(self.JUNK[j][:], in_, AF.Square, reads, list(writes) + [("junk", j)], accum_out=acc)

    def tt(self, eng, out, in0, in1, op, reads, writes):
        e = self.S.eng[eng]
        self.S.add(eng, lambda: e.tensor_tensor(out=out, in0=in0, in1=in1, op=op), reads, writes)

    def ts(self, eng, out, in0, s1, op0, reads, writes, s2=None, op1=None):
        e = self.S.eng[eng]
        if op1 is None:
            self.S.add(eng, lambda: e.tensor_scalar(out=out, in0=in0, scalar1=s1, scalar2=None, op0=op0), reads, writes)
        else:
            self.S.add(eng, lambda: e.tensor_scalar(out=out, in0=in0, scalar1=s1, scalar2=s2, op0=op0, op1=op1),
                       reads, writes)

    def stt(self, out, in0, scalar, in1, op0, op1, reads, writes):
        nc = self.nc
        self.S.add("dve", lambda: nc.vector.scalar_tensor_tensor(out=out, in0=in0, scalar=scalar, in1=in1,
                                                                 op0=op0, op1=op1), reads, writes)

    def cp(self, eng, out, in_, reads, writes):
        if eng == "act":
            self.act(out, in_, AF.Copy, reads, writes)
        else:
            e = self.S.eng[eng]
            self.S.add(eng, lambda: e.tensor_copy(out=out, in_=in_), reads, writes)

    def dma(self, out, in_, reads, writes, key, q="sp"):
        e = self.S.eng[q]
        self.S.add(q, lambda: e.dma_start(out=out, in_=in_), reads, writes, key=key)

    def memset(self, eng, ap, val, writes):
        e = self.S.eng[eng]
        self.S.add(eng, lambda: e.memset(ap, val), [], writes)

    def sb(self, st, name, shape, dt):
        self.uid = getattr(self, "uid", 0) + 1
        return st.enter_context(self.nc.sbuf_tensor("%s_%d" % (name, self.uid), shape, dt))

    def load_weight(self, dst, dst_res, w_ap, col0, ncols, gcol, dst_col0=0, big=False, nslots=8):
        cw = 1024 if big else 512
        for cc in range(ncols // cw):
            for kc in range(KC):
                if big:
                    s = self.wsctr % nslots
                    if s < 4:
                        stg, sres, skey = self.XR[s][:], [("xr", s, 0), ("xr", s, 1)], ("ld", "xrw", s)
                    else:
                        stg, sres, skey = self.XS[:, s - 4, :], [("xsw", s - 4)], ("ld", "xsw", s - 4)
                else:
                    s = self.wsctr % 2
                    stg, sres, skey = self.WS[s][:], [("ws", s)], ("ld", "ws", s)
                self.wsctr += 1
                self.dma(stg, w_ap[kc * 128:(kc + 1) * 128, col0 + cc * cw: col0 + (cc + 1) * cw], [], sres, skey)
                o = dst[:, kc, dst_col0 + cc * cw: dst_col0 + (cc + 1) * cw]
                eng = ("dve", "act", "pool")[self.wsctr % 3]
                rn = [dst_res + (kc, (dst_col0 + cc * cw) // 512 + i) for i in range(cw // 512)]
                if gcol is None:
                    self.cp(eng, o, stg, sres, rn)
                elif eng == "act":
                    self.act(o, stg, AF.Copy, sres, rn, scale=self.PRM[:, gcol + kc: gcol + kc + 1])
                else:
                    self.ts(eng, o, stg, self.PRM[:, gcol + kc: gcol + kc + 1], ALU.mult, sres, rn, s2=0.0, op1=ALU.add)

    def wres(self, name, ncols):
        return [(name, kc, cc) for kc in range(KC) for cc in range(ncols // 512)]

    def norm_sq(self, src, src_res):
        for s in range(4):
            self.sqacc(src[:, s, :], self.SS[:, s:s + 1], [src_res], [("ss", s)])

    def norm_rs(self):
        self.act(self.LNV[:], self.SS[:], AF.Ln, [("ss", s) for s in range(4)], [("lnv",)], scale=1.0 / D, bias=self.EPSC[:, 0:1])
        self.act(self.RSTD[:], self.LNV[:], AF.Exp, [("lnv",)], [("rstd",)], scale=-0.5)

    def norm_stats(self, src, src_res):
        self.norm_sq(src, src_res)
        self.norm_rs()

    def norm_apply_transpose(self, src, src_res, pool_half=False):
        for s in range(4):
            self.norm_apply_sub(src, src_res, s, pool_half)

    def norm_apply_sub(self, src, src_res, s, pool_half=False, part="both"):
        nc = self.nc
        b = s % 2
        if part in ("both", "scale"):
            if pool_half and s % 2 == 1:
                self.ts("pool", self.HTM[b][:], src[:, s, :], self.RSTD[:, s:s + 1], ALU.mult,
                        [src_res, ("rstd",)], [("htm", b)], s2=0.0, op1=ALU.add)
            else:
                self.act(self.HTM[b][:], src[:, s, :], AF.Copy, [src_res, ("rstd",)], [("htm", b)],
                         scale=self.RSTD[:, s:s + 1])
        if part in ("both", "tr"):
            ptv = self.PT[b][:].rearrange("p a (b c) -> p (a b) c", c=128)
            for kc in range(KC):
                o = ptv[:, kc, :]
                i_ = self.HTM[b][:, kc * 128:(kc + 1) * 128]
                self.S.add("pe", lambda o=o, i_=i_: nc.tensor.transpose(o, i_, self.IDENT[:]),
                           [("htm", b)], [("pt", b)])
            self.cp("dve" if (s % 2 == 0 or pool_half) else "act", self.HT[:, :, s * 128:(s + 1) * 128], ptv,
                    [("pt", b)], [(self.HTN, kc) for kc in range(KC)])

    def norm_transpose(self, src, src_res):
        self.norm_stats(src, src_res)
        self.norm_apply_transpose(src, src_res)

    def proj_chunk(self, W, wname, col, b):
        for kc in range(KC):
            self.mm(self.PB[b][:], W[:, kc, col:col + 128], self.HT[:, kc, :], kc == 0, kc == KC - 1,
                    [(self.HTN, kc), (wname, kc, col // 512)], [("pb", b)])

    def res_loads(self, res_d, tok0):
        for s in range(4):
            r0 = tok0 + s * 128
            self.dma(self.XR[s][:], res_d[r0:r0 + 128, :], [], [("xr", s, 0), ("xr", s, 1)], ("ld", "xr", s), q="pool")

    def out_proj(self, WO, wname, res_d, tok0, dst_d, G=None, kc_order=None, preloaded=False):
        order = list(range(KC)) if kc_order is None else kc_order
        used = []
        for s in range(4):
            r = s
            used.append(r)
            XR = self.XR[r]
            r0 = tok0 + s * 128
            if not preloaded:
                self.dma(XR[:], res_d[r0:r0 + 128, :], [], [("xr", r, 0), ("xr", r, 1)], ("ld", "xr", r), q="pool")
            for h in range(2):
                b = self.bank()
                for i, kc in enumerate(order):
                    self.mm(self.PB[b][:], self.MIX[:, kc, s * 128:(s + 1) * 128], WO[:, kc, h * 512:(h + 1) * 512],
                            i == 0, i == KC - 1, [("mix", kc), (wname, kc, h)], [("pb", b)])
                xa = XR[:, h * 512:(h + 1) * 512]
                self.tt("dve", xa, xa, self.PB[b][:], ALU.add, [("pb", b), ("xr", r, h)], [("xr", r, h)])
            if G is None:
                self.dma(dst_d[r0:r0 + 128, :], XR[:], [("xr", r, 0), ("xr", r, 1)], [], ("st", "xr", r), q="pool")
            else:
                self.sqacc(XR[:], self.SS2[:, s:s + 1], [("xr", r, 0), ("xr", r, 1)], [("ss2", s)])
        if G is not None:
            self.act(self.LNV2[:], self.SS2[:], AF.Ln, [("ss2", s) for s in range(4)], [("lnv2",)], scale=1.0 / D,
                     bias=self.EPSC[:, 0:1])
            self.act(self.RSTD2[:], self.LNV2[:], AF.Exp, [("lnv2",)], [("rstd2",)], scale=-0.5)
            for s, r in enumerate(used):
                XR = self.XR[r]
                r0 = tok0 + s * 128
                self.stt(XR[:], XR[:], self.RSTD2[:, s:s + 1], G[:], ALU.mult, ALU.mult,
                         [("xr", r, 0), ("xr", r, 1), ("rstd2",), ("gfin",)], [("xr", r, 0), ("xr", r, 1)])
                self.dma(dst_d[r0:r0 + 128, :], XR[:], [("xr", r, 0), ("xr", r, 1)], [], ("st", "xr", r), q="pool")

    def link_ap(self, b):
        tok = b * T
        if tok == self.ntok_sb // 2:
            return self.PRM[:, P_FLAG + FL_LINK: P_FLAG + FL_LINK + 1]
        if tok == 0 or tok == self.ntok_sb or tok == self.ntok:
            return None
        return self.ONEC[:, 0:1]

    def halos(self, t, CIN, cname, nch, pad):
        cs, ps = t % 2, (t - 1) % 2
        res_c = [(cname, cs, c) for c in range(nch)]
        res_p = [(cname, ps, c) for c in range(nch)]
        lk = self.link_ap(t)
        if t == 0 or lk is None:
            self.memset("pool", CIN[cs][:, :, 0:pad], 0.0, [(cname + "h", cs, "L")])
            if t > 0:
                self.memset("pool", CIN[ps][:, :, pad + T:pad + T + pad], 0.0, [(cname + "h", ps, "R")])
        else:
            self.ts("pool", CIN[ps][:, :, pad + T:pad + T + pad], CIN[cs][:, :, pad:2 * pad], lk, ALU.mult,
                    res_c, [(cname + "h", ps, "R")], s2=0.0, op1=ALU.add)
            self.ts("pool", CIN[cs][:, :, 0:pad], CIN[ps][:, :, T:T + pad], lk, ALU.mult,
                    res_p, [(cname + "h", cs, "L")], s2=0.0, op1=ALU.add)

    def build(self):
        nc = bass.Bass("TRN2", target_bir_lowering=False)
        self.nc = nc
        NTOK = self.ntok
        din = lambda n, shp, dt=F32: nc.dram_tensor(n, shp, dt, kind="ExternalInput").ap()
        x_d = din("x", [NTOK, D])
        prm_d = din("prm", [128, NPRM])
        w_in1 = din("ev_w_in", [D, 2560])
        w_four = din("ev_w_four", [4, 128, 128])
        w_out1 = din("ev_w_out", [D, D])
        w_in2 = din("od_w_in", [D, 4096])
        w_out2 = din("od_w_out", [D, D])
        fin_d = din("final_norm", [D])
        ident_d = din("ident", [128, 128], BF)
        cs_d = din("cs128", [128, 256], BF)
        m1sb_d = din("m1_sb", [128, 256], BF)
        m1u_d = din("m1_u", [128, 256], BF)
        tabsb_d = din("tab_sb", [self.n2_sb, 128, 3 * self.n2_sb], BF)
        tabu_d = din("tab_u", [self.n2_u, 128, 3 * self.n2_u], BF)
        y_d = nc.dram_tensor("y", [NTOK, D], F32, kind="ExternalOutput").ap()
        f_d = nc.dram_tensor("f_scr", [4, 128, NTOK], BF, kind="Internal").ap()
        x1_d = nc.dram_tensor("x1_scr", [NTOK, D], F32, kind="Internal").ap()

        with ExitStack() as top:
            self.S = S = Sched(nc, top)
            sb = self.sb
            self.PRM = sb(top, "prm_sb", [128, NPRM], F32)
            self.IDENT = sb(top, "ident_sb", [128, 128], BF)
            self.ONEC = sb(top, "onec", [128, 1], F32)
            self.EPSC = sb(top, "epsc", [128, 1], F32)
            self.ONEB = sb(top, "oneb", [128, 1], BF)
            self.ONER = sb(top, "oner", [1, 128], F32)
            self.ONEM = sb(top, "onem", [128, 128], BF)
            self.WS = [sb(top, "ws%d" % i, [128, 512], F32) for i in range(2)]
            self.JUNK = [sb(top, "junk%d" % i, [128, D], BF) for i in range(2)]
            self.jctr = 0
            self.SS = sb(top, "ss", [128, 4], F32)
            self.LNV = sb(top, "lnv", [128, 4], F32)
            self.RSTD = sb(top, "rstd", [128, 4], F32)
            self.WM = sb(top, "wm", [128, 4, 6, 128], BF)
            self.PT = [top.enter_context(nc.psum_tensor("pt%d" % i, [128, 2, 512], BF)) for i in range(2)]
            self.PB = [top.enter_context(nc.psum_tensor("pb%d" % i, [128, 512], F32)) for i in range(NPB)]
            self.wsctr = 0

            self.dma(self.PRM[:], prm_d, [], [("prm",)], ("ld", "prm"))
            self.dma(self.IDENT[:], ident_d, [], [("ident",)], ("ld", "ident"))
            self.memset("dve", self.ONEC[:], 1.0, [("onec",)])
            self.memset("dve", self.EPSC[:], EPS, [("epsc",)])
            self.memset("dve", self.ONEB[:], 1.0, [("oneb",)])
            self.memset("dve", self.ONER[:], 1.0, [("oner",)])
            self.memset("dve", self.ONEM[:], 1.0, [("onem",)])
            with ExitStack() as st:
                CS = sb(st, "cs_sb", [128, 256], BF)
                WF4s = sb(st, "wf4s", [128, 4, 128], F32)
                WF4 = sb(st, "wf4", [128, 4, 128], BF)
                self.dma(CS[:], cs_d, [], [("cs",)], ("ld", "cs"))
                self.dma(WF4s[:], w_four.rearrange("g d e -> d g e"), [], [("wf4s",)], ("ld", "wf4s"))
                self.cp("dve", WF4[:], WF4s[:], [("wf4s",)], [("wf4",)])
                for g in range(4):
                    b = self.bank()
                    self.mm(self.PB[b][:, 0:128], CS[:, 0:128], WF4[:, g, :], True, True, [("cs",), ("wf4",)], [("pb", b)])
                    self.mm(self.PB[b][:, 128:256], CS[:, 128:256], WF4[:, g, :], True, True, [("cs",), ("wf4",)], [("pb", b)])
                    for v, fl in enumerate((FL_SELA, FL_SELB, FL_SELU)):
                        for cs_ in range(2):
                            self.ts("dve", self.WM[:, g, 2 * v + cs_, :], self.PB[b][:, cs_ * 128:(cs_ + 1) * 128],
                                    self.PRM[:, P_FLAG + fl:P_FLAG + fl + 1], ALU.mult,
                                    [("pb", b), ("prm",)], [("wm", g, 2 * v + cs_)])
                S.emit()

            with ExitStack() as stf:
                self.WF = self.sb(stf, "wf", [128, KC, 512], BF)
                self.load_weight(self.WF, ("wf",), w_in1, 1024, 512, P_G1)
                self.fourier_block(0, self.ntok_sb, self.n2_sb, True, m1sb_d, tabsb_d, x_d, w_in1, f_d)
                self.fourier_block(self.ntok_sb, self.ntok_u, self.n2_u, False, m1u_d, tabu_d, x_d, w_in1, f_d)

            self.layer1(x_d, w_in1, w_out1, f_d, x1_d)
            self.layer2(x1_d, w_in2, w_out2, fin_d, y_d)
        return nc

    def fourier_block(self, tb, ntok, N2, dual, m1_d, tab_d, x_d, w_in1, f_d):
        nc, S, sb = self.nc, self.S, self.sb
        with ExitStack() as st:
            Fb = sb(st, "Fb", [128, 4, N2, 128], BF)
            with ExitStack() as st1:
                WF = self.WF
                XS2 = [sb(st1, "xsf%d" % i, [128, 4, D], F32) for i in range(2)]
                self.HTM = [sb(st1, "htmf%d" % i, [128, D], BF) for i in range(2)]
                HT2 = [sb(st1, "htf%d" % i, [128, KC, T], BF) for i in range(2)]
                wfres = self.wres("wf", 512) if tb == 0 else []
                xv = x_d[tb:tb + ntok, :].rearrange("(p n) d -> p n d", n=N2)
                NG = N2 // 4
                self.dma(XS2[0][:], xv[:, 0:4, :], [], [("xs", 0)], ("ld", "xs", 0))
                for q in range(NG):
                    xq = q % 2
                    if q + 1 < NG:
                        self.dma(XS2[1 - xq][:], xv[:, 4 * q + 4:4 * q + 8, :], [], [("xs", 1 - xq)], ("ld", "xs", 1 - xq))
                    self.HT, self.HTN = HT2[xq], "htf%d" % xq
                    self.norm_stats(XS2[xq], ("xs", xq))
                    self.norm_apply_transpose(XS2[xq], ("xs", xq), pool_half=True)
                    for i in range(4):
                        n2 = 4 * q + i
                        b = self.bank()
                        for kc in range(KC):
                            self.mm(self.PB[b][:], self.HT[:, kc, i * 128:(i + 1) * 128], WF[:, kc, :], kc == 0, kc == KC - 1,
                                    [(self.HTN, kc)] + wfres, [("pb", b)])
                        self.cp("dve", Fb[:, :, n2, :], self.PB[b][:].rearrange("p (g c) -> p g c", c=128),
                                [("pb", b)], [("F", n2)])
                S.emit()
                self.HTN = "ht"
            Fres = [("F", n2) for n2 in range(N2)]
            A = sb(st, "Adft", [N2, 128, 256], BF)
            Yr = sb(st, "Yr", [128, N2, 128], BF)
            Yi = sb(st, "Yi", [128, N2, 128], BF)
            M1 = sb(st, "m1", [128, 256], BF)
            JC = 16
            TAB = [sb(st, "tab%d" % i, [N2, JC, 3 * N2], BF) for i in range(2)]
            FST = [sb(st, "fst%d" % i, [128, 512], BF) for i in range(2)]
            self.dma(M1[:], m1_d, [], [("m1",)], ("ld", "m1"))
            JB = min(512 // (2 * N2), JC)
            tabctr = 0
            evc = 0
            for g in range(4):
                for cp_ in range(64):
                    b = self.bank()
                    for k in range(2):
                        c = 2 * cp_ + k
                        self.mm(self.PB[b][0:N2, k * 256:(k + 1) * 256], Fb[:, g, :, c], M1[:], True, True,
                                Fres + [("m1",)], [("pb", b)])
                    evc += 1
                    self.cp("act" if evc % 2 else "dve", A[:, 2 * cp_:2 * cp_ + 2, :],
                            self.PB[b][0:N2, :].rearrange("p (k j) -> p k j", k=2), [("pb", b)], [("A", cp_)])
                Ares = [("A", i) for i in range(64)]
                for jc in range(128 // JC):
                    tsl = tabctr % 2
                    tabctr += 1
                    self.dma(TAB[tsl][:], tab_d[:, jc * JC:(jc + 1) * JC, :], [], [("tab", tsl)], ("ld", "tab", tsl))
                    for jb in range(JC // JB):
                        b = self.bank()
                        for jq in range(JB):
                            jj = jb * JB + jq
                            j = jc * JC + jj
                            o = self.PB[b][:, jq * 2 * N2:(jq + 1) * 2 * N2]
                            self.mm(o, A[:, :, j], TAB[tsl][:, jj, N2:3 * N2], True, False, Ares + [("tab", tsl)], [("pb", b)])
                            self.mm(o, A[:, :, 128 + j], TAB[tsl][:, jj, 0:2 * N2], False, True, Ares + [("tab", tsl)], [("pb", b)])
                        j0 = jc * JC + jb * JB
                        pv = self.PB[b][:, 0:JB * 2 * N2].rearrange("p (j r m) -> p r m j", r=2, m=N2)
                        evc += 1
                        self.cp("act" if evc % 2 else "dve", Yr[:, :, j0:j0 + JB], pv[:, 0], [("pb", b)], [("Y", 0, j0)])
                        self.cp("act" if evc % 2 else "dve", Yi[:, :, j0:j0 + JB], pv[:, 1], [("pb", b)], [("Y", 1, j0)])
                Yres = [("Y", r, j0) for r in range(2) for j0 in range(0, 128, JB)]
                rowlen = N2 * 128
                for pt_ in range(ntok // 512):
                    P0 = pt_ * 512
                    b = self.bank()
                    if dual:
                        Sh = ntok // 2
                        hp = P0 // Sh
                        q = (P0 - Sh * hp) // 512
                        offB = 128 * 8 * q + 64 * hp
                        apB = [[rowlen, 128], [128, 8], [1, 64]]
                        ops_ = [(0, Yr[:, :, :].rearrange("p m j -> p (m j)")[:, P0:P0 + 512]),
                                (1, Yi[:, :, :].rearrange("p m j -> p (m j)")[:, P0:P0 + 512]),
                                (2, bass.AP(Yr, offB, apB)), (3, bass.AP(Yi, offB, apB))]
                    else:
                        ops_ = [(4, Yr[:, :, :].rearrange("p m j -> p (m j)")[:, P0:P0 + 512]),
                                (5, Yi[:, :, :].rearrange("p m j -> p (m j)")[:, P0:P0 + 512])]
                    for i, (v, rhs) in enumerate(ops_):
                        self.mm(self.PB[b][:], self.WM[:, g, v, :], rhs, i == 0, i == len(ops_) - 1,
                                Yres + [("wm", g, v)], [("pb", b)])
                    fs = pt_ % 2
                    evc += 1
                    self.cp("act" if evc % 2 else "dve", FST[fs][:], self.PB[b][:], [("pb", b)], [("fst", fs)])
                    self.dma(f_d[g, :, tb + P0: tb + P0 + 512], FST[fs][:], [("fst", fs)], [], ("st", "fst", fs))
            S.emit()

    def layer1(self, x_d, w_in1, w_out1, f_d, x1_d):
        nc, S, sb = self.nc, self.S, self.sb
        NT = self.nt
        with ExitStack() as st:
            W2 = sb(st, "w2", [128, KC, 2048], BF)
            WO = sb(st, "wo1", [128, KC, D], BF)
            D31 = sb(st, "d31", [128, 4, 31, 128], BF)
            self.XS = sb(st, "xs", [128, 4, D], F32)
            self.XR = [sb(st, "xr%d" % i, [128, D], F32) for i in range(4)]
            self.xrctr = 0
            self.HTM = [sb(st, "htm%d" % i, [128, D], BF) for i in range(2)]
            self.HT = sb(st, "ht", [128, KC, T], BF)
            CIN = [sb(st, "cin%d" % i, [128, 4, T + 30], BF) for i in range(2)]
            SZ = [sb(st, "sz%d" % i, [128, 8, T], BF) for i in range(2)]
            SIG = [sb(st, "sig%d" % i, [128, T], BF) for i in range(2)]
            CO = sb(st, "co", [128, 4, T], F32)
            COB = [sb(st, "cob%d" % i, [128, T], BF) for i in range(2)]
            SQ = [sb(st, "sq%d" % i, [128, T], BF) for i in range(2)]
            ROW = sb(st, "row", [128, T], F32)
            ROW2 = sb(st, "row2", [128, T], F32)
            LT = [sb(st, "lt%d" % i, [128, T], F32) for i in range(2)]
            LS = [sb(st, "ls%d" % i, [128, T], BF) for i in range(2)]
            self.MIX = sb(st, "mix", [128, 8, T], BF)
            FSL = sb(st, "fsl", [128, 4, T], BF)
            PRM = self.PRM
            self.dma(self.XS[:], x_d[0:T, :].rearrange("(s p) d -> p s d", p=128), [],
                     [("xs",)] + [("xsw", i) for i in range(4)], ("ld", "xs"))
            self.load_weight(W2, ("w2",), w_in1, 0, 1024, P_G1, 0, big=True, nslots=4)
            def build_d31():
                for c in range(4):
                    ident_b = bass.AP(self.IDENT, 0, [[128, 128], [0, 31], [1, 128]])
                    w_b = bass.AP(self.PRM, P_W31 + c * 31, [[NPRM, 128], [1, 31], [0, 128]])
                    self.stt(D31[:, c, :, :], ident_b, 0.5, w_b, ALU.mult, ALU.mult, [], [("d31", c)])

            for i in range(2):
                self.memset("pool", CIN[i][:], 0.0, [("cin", i, c) for c in range(4)] + [("cinh", i, "L"), ("cinh", i, "R")])
            st2 = {}

            def xload(t):
                tok0 = t * T
                self.dma(self.XS[:], x_d[tok0:tok0 + T, :].rearrange("(s p) d -> p s d", p=128), [],
                         [("xs",)] + [("xsw", i) for i in range(4)], ("ld", "xs"))

            def stage1a(t):
                cs = t % 2
                for c in range(4):
                    bg = self.bank()
                    self.proj_chunk(W2, "w2", 512 + c * 128, bg)
                    self.act(SIG[c % 2][:], self.PB[bg][:], AF.Tanh, [("pb", bg)], [("sig", c % 2)], scale=0.5)
                    bv = self.bank()
                    self.proj_chunk(W2, "w2", c * 128, bv)
                    self.stt(CIN[cs][:, c, 15:15 + T], SIG[c % 2][:], 1.0, self.PB[bv][:], ALU.add, ALU.mult,
                             [("pb", bv), ("sig", c % 2)], [("cin", cs, c)])

            def stage1z(t, lo, hi):
                cs = t % 2
                for c in range(lo, hi):
                    bz = self.bank()
                    self.proj_chunk(W2, "w2", 1024 + c * 128, bz)
                    self.act(SZ[cs][:, c, :], self.PB[bz][:], AF.Silu, [("pb", bz)], [("sz", cs, c)])

            def stage2a(t):
                cs = t % 2
                tok0 = t * T
                self.dma(FSL[:], f_d[:, :, tok0:tok0 + T].rearrange("g p n -> p g n"), [], [("fsl",)], ("ld", "fsl"))
                cb = [self.bank() for _ in range(4)]
                bm = self.bank()
                bq = self.bank()

                def stats(c):
                    k = c % 2
                    self.mm(self.PB[bm][:], self.ONEM[:], COB[k][:], c == 0, c == 3, [("cob", k), ("onem",)], [("pb", bm)])
                    self.mm(self.PB[bq][:], self.ONEM[:], SQ[k][:], c == 0, c == 3, [("sq", k), ("onem",)], [("pb", bq)])

                for c in range(4):
                    b = cb[c]
                    k = c % 2
                    for tap in range(31):
                        self.mm(self.PB[b][:], D31[:, c, tap, :], CIN[cs][:, c, tap:tap + T], tap == 0, tap == 30,
                                [("cin", cs, c), ("cinh", cs, "L"), ("cinh", cs, "R"), ("d31", c)], [("pb", b)])
                    if c >= 1:
                        stats(c - 1)
                    bia = PRM[:, P_B31 + c:P_B31 + c + 1]
                    self.act(CO[:, c, :], self.PB[b][:], AF.Identity, [("pb", b)], [("co", c)], bias=bia)
                    self.act(COB[k][:], self.PB[b][:], AF.Identity, [("pb", b)], [("cob", k)], bias=bia)
                    self.act(SQ[k][:], self.PB[b][:], AF.Square, [("pb", b)], [("sq", k)], bias=bia)
                st2["stats3"] = lambda: stats(3)
                st2["bm"], st2["bq"] = bm, bq

            def stage2fin(t):
                st2["stats3"]()
                bm, bq = st2["bm"], st2["bq"]
                self.act(ROW[:], self.PB[bm][:], AF.Square, [("pb", bm)], [("row", 0)], scale=1.0 / 512)
                self.stt(ROW[:], self.PB[bq][:], 1.0 / 512, ROW[:], ALU.mult, ALU.subtract, [("pb", bq), ("row", 0)], [("row", 0)])
                if st2.get("pending_rs"):
                    self.norm_rs()
                    st2["pending_rs"] = False
                self.act(ROW[:], ROW[:], AF.Ln, [("row", 0)], [("row", 0)], bias=self.EPSC[:, 0:1])
                self.act(ROW[:], ROW[:], AF.Exp, [("row", 0)], [("row", 0)], scale=-0.5)
                self.stt(ROW2[:], self.PB[bm][:], -1.0 / 512, ROW[:], ALU.mult, ALU.mult, [("pb", bm), ("row", 0)], [("row", 1)])

            def stage2bc(t):
                pass

            def stage2ln(t, c0, c1):
                cs = t % 2
                for c in range(c0, c1):
                    k = c % 2
                    self.tt("dve", LT[k][:], CO[:, c, :], ROW[:], ALU.mult, [("co", c), ("row", 0)], [("lt", k)])
                    self.tt("dve", LT[k][:], LT[k][:], ROW2[:], ALU.add, [("lt", k), ("row", 1)], [("lt", k)])
                    self.act(LS[k][:], LT[k][:], AF.Silu, [("lt", k)], [("ls", k)],
                             scale=PRM[:, P_LNG + c:P_LNG + c + 1], bias=PRM[:, P_LNB + c:P_LNB + c + 1])
                    self.tt("dve", self.MIX[:, c, :], LS[k][:], SZ[cs][:, c, :], ALU.mult,
                            [("ls", k), ("sz", cs, c)], [("mix", c)])

            def stage2gate(t):
                cs = t % 2
                for g in range(4):
                    self.tt("pool", self.MIX[:, 4 + g, :], FSL[:, g, :], SZ[cs][:, 4 + g, :], ALU.mult,
                            [("fsl",), ("sz", cs, 4 + g)], [("mix", 4 + g)])

            def stage2op(t):
                self.out_proj(WO, "wo", x_d, t * T, x1_d, preloaded=True)

            self.norm_stats(self.XS, ("xs",))
            self.norm_apply_transpose(self.XS, ("xs",))
            if NT > 1:
                xload(1)
            for t in range(NT + 2):
                if t < NT:
                    stage1a(t)
                    self.halos(t, CIN, "cin", 4, 15)
                elif t == NT:
                    self.memset("pool", CIN[(t - 1) % 2][:, :, 15 + T:30 + T], 0.0, [("cinh", (t - 1) % 2, "R")])
                if t == 0:
                    build_d31()
                    self.load_weight(W2, ("w2",), w_in1, 1536, 1024, P_G1, 1024, big=True, nslots=4)
                if t >= 2:
                    stage2op(t - 2)
                if t < NT:
                    stage1z(t, 0, 4)
                if t == 0:
                    self.load_weight(WO, ("wo",), w_out1, 0, D, None, big=True, nslots=4)
                s2 = 1 <= t <= NT
                if t + 1 < NT:
                    self.norm_sq(self.XS, ("xs",))
                    st2["pending_rs"] = True
                if s2:
                    stage2a(t - 1)
                    self.res_loads(x_d, (t - 1) * T)
                if t < NT:
                    stage1z(t, 4, 5)
                if s2:
                    stage2fin(t - 1)
                if st2.get("pending_rs"):
                    self.norm_rs()
                    st2["pending_rs"] = False
                if t + 1 < NT:
                    self.norm_apply_sub(self.XS, ("xs",), 0, True, "scale")
                    self.norm_apply_sub(self.XS, ("xs",), 1, True, "scale")
                if t < NT:
                    stage1z(t, 5, 8)
                if t + 1 < NT:
                    self.norm_apply_sub(self.XS, ("xs",), 0, True, "tr")
                    self.norm_apply_sub(self.XS, ("xs",), 1, True, "tr")
                    self.norm_apply_sub(self.XS, ("xs",), 2, True)
                    self.norm_apply_sub(self.XS, ("xs",), 3, True)
                    if t + 2 < NT:
                        xload(t + 2)
                if s2:
                    stage2bc(t - 1)
                    stage2gate(t - 1)
                    stage2ln(t - 1, 0, 4)
            S.emit()

    def layer2(self, x1_d, w_in2, w_out2, fin_d, y_d):
        nc, S, sb = self.nc, self.S, self.sb
        NT = self.nt
        with ExitStack() as st:
            W3 = sb(st, "w3", [128, KC, 4096], BF)
            WO = sb(st, "wo2", [128, KC, D], BF)
            D3 = sb(st, "d3", [128, 8, 3, 128], BF)
            G = sb(st, "gfin", [128, D], F32)
            self.XS = sb(st, "xsb", [128, 4, D], F32)
            self.XR = [sb(st, "xrb%d" % i, [128, D], F32) for i in range(4)]
            self.xrctr = 0
            self.SS2 = sb(st, "ss2", [128, 4], F32)
            self.LNV2 = sb(st, "lnv2", [128, 4], F32)
            self.RSTD2 = sb(st, "rstd2", [128, 4], F32)
            self.HTM = [sb(st, "htmb%d" % i, [128, D], BF) for i in range(2)]
            self.HT = sb(st, "htb", [128, KC, T], BF)
            CIN = [sb(st, "cu%d" % i, [128, 8, T + 2], BF) for i in range(2)]
            BGZ = [sb(st, "bgz%d" % i, [128, 8, T], BF) for i in range(2)]
            UT = [sb(st, "ut%d" % i, [128, T], BF) for i in range(2)]
            ZT = [sb(st, "zt%d" % i, [128, T], BF) for i in range(2)]
            self.MIX = sb(st, "mixb", [128, 8, T], BF)
            PRM = self.PRM
            self.dma(self.XS[:], x1_d[0:T, :].rearrange("(s p) d -> p s d", p=128), [],
                     [("xs",)] + [("xsw", i) for i in range(4)], ("ld", "xs"))
            for c0 in (0, 2048):
                self.load_weight(W3, ("w3",), w_in2, c0, 1024, P_G2, c0, big=True, nslots=4)
            self.dma(G[:], fin_d.partition_broadcast(128), [], [("gfin",)], ("ld", "gfin"))
            def build_d3():
                ident_b = bass.AP(self.IDENT, 0, [[128, 128], [0, 24], [1, 128]])
                w_b = bass.AP(self.PRM, P_W3, [[NPRM, 128], [1, 24], [0, 128]])
                self.stt(D3[:, :, :, :].rearrange("p c t m -> p (c t) m"), ident_b, 1.0, w_b, ALU.mult, ALU.mult,
                         [], [("d3", c) for c in range(8)])

            for i in range(2):
                self.memset("pool", CIN[i][:], 0.0, [("cu", i, c) for c in range(8)] + [("cuh", i, "L"), ("cuh", i, "R")])

            def xload(t):
                tok0 = t * T
                self.dma(self.XS[:], x1_d[tok0:tok0 + T, :].rearrange("(s p) d -> p s d", p=128), [],
                         [("xs",)] + [("xsw", i) for i in range(4)], ("ld", "xs"))

            def stage1a(t):
                cs = t % 2
                for c in range(8):
                    k = c % 2
                    bu = self.bank()
                    self.proj_chunk(W3, "w3", c * 128, bu)
                    self.act(UT[k][:], self.PB[bu][:], AF.Copy, [("pb", bu)], [("ut", k)])
                    bc = self.bank()
                    self.proj_chunk(W3, "w3", 2048 + c * 128, bc)
                    self.tt("dve", CIN[cs][:, c, 1:1 + T], self.PB[bc][:], UT[k][:], ALU.mult,
                            [("pb", bc), ("ut", k)], [("cu", cs, c)])

            def stage1z(t):
                cs = t % 2
                for c in range(8):
                    k = c % 2
                    bz = self.bank()
                    self.proj_chunk(W3, "w3", 3072 + c * 128, bz)
                    self.act(ZT[k][:], self.PB[bz][:], AF.Silu, [("pb", bz)], [("zt", k)])
                    bb = self.bank()
                    self.proj_chunk(W3, "w3", 1024 + c * 128, bb)
                    self.tt("dve", BGZ[cs][:, c, :], self.PB[bb][:], ZT[k][:], ALU.mult,
                            [("pb", bb), ("zt", k)], [("bgz", cs, c)])

            def stage2c(t):
                cs = t % 2
                for c in range(8):
                    b = self.bank()
                    for tap in range(3):
                        self.mm(self.PB[b][:], D3[:, c, tap, :], CIN[cs][:, c, tap:tap + T], tap == 0, tap == 2,
                                [("cu", cs, c), ("cuh", cs, "L"), ("cuh", cs, "R"), ("d3", c)], [("pb", b)])
                    self.stt(self.MIX[:, c, :], self.PB[b][:], PRM[:, P_B3 + c:P_B3 + c + 1], BGZ[cs][:, c, :],
                             ALU.add, ALU.mult, [("pb", b), ("bgz", cs, c)], [("mix", c)])

            def stage2o(t):
                tok0 = t * T
                self.out_proj(WO, "wo", x1_d, tok0, y_d, G=G)

            self.norm_stats(self.XS, ("xs",))
            self.norm_apply_transpose(self.XS, ("xs",))
            if NT > 1:
                xload(1)
            for t in range(NT + 1):
                if t < NT:
                    stage1a(t)
                    self.halos(t, CIN, "cu", 8, 1)
                else:
                    self.memset("pool", CIN[(t - 1) % 2][:, :, 1 + T:2 + T], 0.0, [("cuh", (t - 1) % 2, "R")])
                if t == 0:
                    build_d3()
                    for c0 in (3072, 1024):
                        self.load_weight(W3, ("w3",), w_in2, c0, 1024, P_G2, c0, big=True, nslots=4)
                if t + 1 < NT:
                    self.norm_stats(self.XS, ("xs",))
                if t >= 1:
                    stage2c(t - 1)
                if t + 1 < NT:
                    self.norm_apply_sub(self.XS, ("xs",), 0, True, "scale")
                    self.norm_apply_sub(self.XS, ("xs",), 1, True, "scale")
                if t < NT:
                    stage1z(t)
                if t == 0:
                    self.load_weight(WO, ("wo",), w_out2, 0, D, None, big=True, nslots=4)
                if t + 1 < NT:
                    self.norm_apply_sub(self.XS, ("xs",), 0, True, "tr")
                    self.norm_apply_sub(self.XS, ("xs",), 1, True, "tr")
                    self.norm_apply_sub(self.XS, ("xs",), 2, True)
                    self.norm_apply_sub(self.XS, ("xs",), 3, True)
                    if t + 2 < NT:
                        xload(t + 2)
                if t >= 1:
                    stage2o(t - 1)
            S.emit()


def dft_tables(N2, mode):
    n1 = np.arange(128)[:, None]
    j = np.arange(128)[None, :]
    n2 = np.arange(N2)[:, None, None]
    jj = np.arange(128)[None, :, None]
    m = np.arange(N2)[None, None, :]
    if mode == "A":
        ang = 2 * np.pi * n1 * j / 128
        M1 = np.concatenate([np.cos(ang), -np.sin(ang)], 1)
        th = 2 * np.pi * n2 * (jj + 128 * m) / (128 * N2)
    else:
        mask = ((n1 // 64) == (j // 64))
        ang = 2 * np.pi * (n1 % 64) * (j % 64) / 64
        M1 = np.concatenate([np.cos(ang) * mask, -np.sin(ang) * mask], 1)
        th = 2 * np.pi * n2 * ((jj % 64) + 64 * m) / (64 * N2)
    Wr, Wi = np.cos(th), -np.sin(th)
    Tab = np.concatenate([-Wi, Wr, Wi], 2)
    return M1.astype(ml_dtypes.bfloat16), np.ascontiguousarray(Tab).astype(ml_dtypes.bfloat16)


def pack_prm(inp, mode, ntok_sb, ntok_u):
    prm = np.zeros((128, NPRM), np.float32)
    col = lambda v: np.ascontiguousarray(np.asarray(v, np.float32).reshape(-1, 128).T)
    prm[:, P_G1:P_G1 + 8] = col(inp["ev_norm"][0])
    prm[:, P_G2:P_G2 + 8] = col(inp["od_norm"][0])
    prm[:, P_B31:P_B31 + 4] = col(inp["ev_conv_b"][0])
    prm[:, P_LNG:P_LNG + 4] = col(inp["ev_ln_g"][0])
    prm[:, P_LNB:P_LNB + 4] = col(inp["ev_ln_b"][0])
    prm[:, P_B3:P_B3 + 8] = col(inp["od_conv_b"][0])
    w31 = np.asarray(inp["ev_conv_w"][0], np.float32)
    prm[:, P_W31:P_W31 + 124] = w31.T.reshape(4, 128, 31).transpose(1, 0, 2).reshape(128, 124)
    w3 = np.asarray(inp["od_conv_w"][0], np.float32)
    prm[:, P_W3:P_W3 + 24] = w3.T.reshape(8, 128, 3).transpose(1, 0, 2).reshape(128, 24)
    prm[:, P_FLAG + FL_SELA] = (1.0 / np.sqrt(128.0 * ntok_sb)) if mode == "A" else 0.0
    prm[:, P_FLAG + FL_SELB] = (1.0 / np.sqrt(128.0 * ntok_sb / 2)) if mode == "B" else 0.0
    prm[:, P_FLAG + FL_SELU] = 1.0 / np.sqrt(128.0 * ntok_u)
    prm[:, P_FLAG + FL_LINK] = 1.0 if mode == "A" else 0.0
    return prm


_NC_CACHE = {}


def run_cores(core_x, core_modes, inp, ntok_sb, n2_sb, ntok_u, n2_u):
    key = (ntok_sb, n2_sb, ntok_u, n2_u)
    if key not in _NC_CACHE:
        _NC_CACHE[key] = Builder(ntok_sb, n2_sb, ntok_u, n2_u).build()
    nc = _NC_CACHE[key]
    c = np.arange(128)
    ang = 2 * np.pi * np.outer(c, c) / 128
    cs128 = np.concatenate([np.cos(ang), np.sin(ang)], 1).astype(ml_dtypes.bfloat16)
    ident = np.eye(128, dtype=np.float32).astype(ml_dtypes.bfloat16)
    tabs = {m: dft_tables(n2_sb, m) for m in set(core_modes)}
    m1u, tabu = dft_tables(n2_u, "A")
    f32 = lambda a: np.ascontiguousarray(np.asarray(a, np.float32))
    in_maps = []
    for xc, mode in zip(core_x, core_modes):
        in_maps.append({
            "x": f32(xc), "prm": pack_prm(inp, mode, ntok_sb, ntok_u),
            "ev_w_in": f32(inp["ev_w_in"][0]), "ev_w_four": f32(inp["ev_w_four"][0]),
            "ev_w_out": f32(inp["ev_w_out"][0]), "od_w_in": f32(inp["od_w_in"][0]),
            "od_w_out": f32(inp["od_w_out"][0]), "final_norm": f32(inp["final_norm"]),
            "ident": ident, "cs128": cs128, "m1_sb": tabs[mode][0], "tab_sb": tabs[mode][1],
            "m1_u": m1u, "tab_u": tabu,
        })
    res = run_bass_kernel_spmd(nc, in_maps, core_ids=list(range(len(in_maps))))
    return [np.asarray(r["y"], np.float32) for r in res.results]


def kernel(x_prompt, x_sample, **w):
    xp = np.asarray(x_prompt, np.float32)
    xs = np.asarray(x_sample, np.float32)
    core_x, modes, plan = [], [], []
    for i in range(4):
        core_x.append(np.concatenate([xs[i], xp[i]], 0))
        modes.append("A")
        plan.append([("s", i), ("p", i)])
    for i in range(4):
        ids = [4 + 3 * i, 5 + 3 * i, 6 + 3 * i]
        core_x.append(np.concatenate([xp[k] for k in ids], 0))
        modes.append("B")
        plan.append([("p", k) for k in ids])
    outs = run_cores(core_x, modes, w, 8192, 64, 4096, 32)
    yp = np.empty_like(xp)
    ys = np.empty_like(xs)
    for o, pl in zip(outs, plan):
        off = 0
        for kind, k in pl:
            if kind == "s":
                ys[k] = o[off:off + 8192]
                off += 8192
            else:
                yp[k] = o[off:off + 4096]
                off += 4096
    return (yp, ys)
```
